# Optimizing a Trainium2 kernel written in Bass

```python
import math
import jax
import jax.numpy as jnp
from jax import lax
import numpy as np

D_MODEL = 1024
BATCH = 8
SEQ = 4096
DEPTH = 2

HEAD_DIM = D_MODEL // 16
NSA_HEADS = 6
NSA_KV_GROUPS = 2
NSA_HPG = NSA_HEADS // NSA_KV_GROUPS
CMP_BLOCK = 32
CMP_STRIDE = 16
CMP_HIDDEN = 2 * HEAD_DIM
SLC_BLOCK = 64
SLC_TOPN = 16
WINDOW = 512
NSA_QBLOCK = 64
FORCE_BONUS = 1000.0
NEG_BIG = -1e30
LB_FLOOR = 1e-30
SB_HEADS = 4
SB_QBLOCK = 128
HG_HEADS = 6
HG_KDIM = HEAD_DIM
HG_VDIM = HEAD_DIM
HG_CHUNK = 64
NUM_BUCKETS = 32
MAX_DISTANCE = 128
D_NSA = NSA_HEADS * HEAD_DIM
D_KV = NSA_KV_GROUPS * HEAD_DIM
D_SB = SB_HEADS * HEAD_DIM
D_HG = HG_HEADS * HG_VDIM
D_MIX = D_NSA + D_SB + D_HG
N_GATES = NSA_HEADS * 3
SPLIT_SIZES = (D_NSA, D_KV, D_KV, D_KV, D_KV, D_KV, D_KV, N_GATES, D_NSA,
               D_SB, D_SB, D_SB, D_SB,
               HG_HEADS * HG_KDIM, HG_HEADS * HG_KDIM, D_HG, D_HG)
D_IN = sum(SPLIT_SIZES)
ALPHA = (2 * DEPTH) ** 0.25
OUT_INIT_SCALE = (8 * DEPTH) ** -0.25
LN_EPS = 1e-5
RMS_EPS = 1e-6

kernel_name = 'hybrid_nsa_stickbreak_hgrn2_deepnorm'


def t5_bucket(rel):
    n = jnp.maximum(rel, 0)
    max_exact = NUM_BUCKETS // 2
    large = max_exact + (jnp.log(jnp.maximum(n, 1).astype(jnp.float32) / max_exact)
                         / math.log(MAX_DISTANCE / max_exact)
                         * (NUM_BUCKETS - max_exact)).astype(jnp.int32)
    large = jnp.clip(large, 0, NUM_BUCKETS - 1)
    return jnp.where(n < max_exact, n, large)


def masked_softmax(s, mask):
    s = jnp.where(mask, s.astype(jnp.float32), NEG_BIG)
    p = jax.nn.softmax(s, axis=-1)
    return jnp.where(mask, p, 0.0)


def layer_norm(v, g, b):
    v32 = v.astype(jnp.float32)
    mu = jnp.mean(v32, axis=-1, keepdims=True)
    var = jnp.mean(jnp.square(v32 - mu), axis=-1, keepdims=True)
    return ((v32 - mu) * lax.rsqrt(var + LN_EPS) * g + b).astype(v.dtype)


def nsa_mixer(q, kc_src, vc_src, ks_src, vs_src, kw_src, vw_src, gate_logits,
              cmp_pos, w_ck1, w_ck2, w_cv1, w_cv2, rel_bias):
    B, S = q.shape[:2]
    G, HPG, dk, QB = NSA_KV_GROUPS, NSA_HPG, HEAD_DIM, NSA_QBLOCK
    scale = 1.0 / math.sqrt(dk)
    q = q.reshape(B, S, G, HPG, dk).transpose(0, 2, 3, 1, 4)
    gates = jax.nn.sigmoid(gate_logits.reshape(B, S, G, HPG, 3).transpose(0, 2, 3, 1, 4))

    n_cmp = (S - CMP_BLOCK) // CMP_STRIDE + 1
    blk_idx = np.arange(n_cmp)[:, None] * CMP_STRIDE + np.arange(CMP_BLOCK)[None, :]

    def compress(src, w1, w2):
        blocks = src.reshape(B, S, G, dk)[:, blk_idx] + cmp_pos[None, None, :, None, :]
        blocks = blocks.transpose(0, 3, 1, 2, 4).reshape(B, G, n_cmp, CMP_BLOCK * dk)
        return jax.nn.gelu(blocks @ w1) @ w2

    kc = compress(kc_src, w_ck1, w_ck2)
    vc = compress(vc_src, w_cv1, w_cv2)
    cmp_start = np.arange(n_cmp) * CMP_STRIDE
    cmp_end = jnp.asarray(cmp_start + CMP_BLOCK - 1, dtype=jnp.int32)

    n_slc = S // SLC_BLOCK
    topn = min(SLC_TOPN, n_slc)
    slc_start = np.arange(n_slc) * SLC_BLOCK
    overlap = jnp.asarray(((cmp_start[:, None] < slc_start[None, :] + SLC_BLOCK)
                           & (cmp_start[:, None] + CMP_BLOCK > slc_start[None, :])).astype(np.float32))
    ks = ks_src.reshape(B, S, G, dk).transpose(0, 2, 1, 3).reshape(B, G, n_slc, SLC_BLOCK, dk)
    vs = vs_src.reshape(B, S, G, dk).transpose(0, 2, 1, 3).reshape(B, G, n_slc, SLC_BLOCK, dk)

    pad = ((0, 0), (0, 0), (WINDOW, 0), (0, 0))
    kw = jnp.pad(kw_src.reshape(B, S, G, dk).transpose(0, 2, 1, 3), pad)
    vw = jnp.pad(vw_src.reshape(B, S, G, dk).transpose(0, 2, 1, 3), pad)

    tbl = rel_bias.reshape(NUM_BUCKETS, G, HPG)
    tbl_g = tbl.transpose(1, 0, 2)
    b_ix = jnp.arange(B)[:, None, None, None]
    g_ix = jnp.arange(G)[None, :, None, None]
    blk = jnp.arange(n_slc)

    def block(i):
        t0 = i * QB
        t = t0 + jnp.arange(QB)
        qb = lax.dynamic_slice_in_dim(q, t0, QB, axis=3)
        gb = lax.dynamic_slice_in_dim(gates, t0, QB, axis=3)

        rel_c = t[:, None] - cmp_end[None, :]
        bias_c = tbl[t5_bucket(rel_c)].transpose(2, 3, 0, 1)
        s_c = jnp.einsum('bghtd,bgnd->bghtn', qb, kc) * scale + bias_c
        p_c = masked_softmax(s_c, rel_c >= 0)
        o_c = jnp.einsum('bghtn,bgnd->bghtd', p_c.astype(vc.dtype), vc)

        imp = jnp.einsum('bghtn,nj->bgtj', p_c, overlap)
        cur = t // SLC_BLOCK
        forced = (blk[None, :] == 0) | (blk[None, :] == cur[:, None]) | (blk[None, :] == cur[:, None] - 1)
        valid = blk[None, :] * SLC_BLOCK <= t[:, None]
        score = jnp.where(valid, imp + FORCE_BONUS * forced.astype(jnp.float32), NEG_BIG)
        _, idx = lax.top_k(score, topn)
        k_sel = ks[b_ix, g_ix, idx].reshape(B, G, QB, topn * SLC_BLOCK, dk)
        v_sel = vs[b_ix, g_ix, idx].reshape(B, G, QB, topn * SLC_BLOCK, dk)
        pos = (idx[..., None] * SLC_BLOCK + jnp.arange(SLC_BLOCK)).reshape(B, G, QB, topn * SLC_BLOCK)
        rel_s = t[:, None] - pos
        bias_s = jnp.moveaxis(tbl_g[g_ix, t5_bucket(rel_s)], -1, 2)
        s_s = jnp.einsum('bghtd,bgtkd->bghtk', qb, k_sel) * scale + bias_s
        p_s = masked_softmax(s_s, (rel_s >= 0)[:, :, None])
        o_s = jnp.einsum('bghtk,bgtkd->bghtd', p_s.astype(v_sel.dtype), v_sel)

        kwb = lax.dynamic_slice_in_dim(kw, t0, WINDOW + QB, axis=2)
        vwb = lax.dynamic_slice_in_dim(vw, t0, WINDOW + QB, axis=2)
        pos_w = t0 - WINDOW + jnp.arange(WINDOW + QB)
        rel_w = t[:, None] - pos_w[None, :]
        mask_w = (rel_w >= 0) & (rel_w < WINDOW) & (pos_w[None, :] >= 0)
        bias_w = tbl[t5_bucket(rel_w)].transpose(2, 3, 0, 1)
        s_w = jnp.einsum('bghtd,bgkd->bghtk', qb, kwb) * scale + bias_w
        p_w = masked_softmax(s_w, mask_w)
        o_w = jnp.einsum('bghtk,bgkd->bghtd', p_w.astype(vwb.dtype), vwb)

        return gb[..., 0:1] * o_c + gb[..., 1:2] * o_s + gb[..., 2:3] * o_w

    o = lax.map(block, jnp.arange(S // QB))
    return o.transpose(1, 0, 4, 2, 3, 5).reshape(B, S, D_NSA)


def stick_breaking_mixer(q, k, v):
    B, S = q.shape[:2]
    H, dk, QB = SB_HEADS, HEAD_DIM, SB_QBLOCK
    scale = 1.0 / math.sqrt(dk)
    q = q.reshape(B, S, H, dk).transpose(0, 2, 1, 3)
    k = k.reshape(B, S, H, dk).transpose(0, 2, 1, 3)
    v = v.reshape(B, S, H, dk).transpose(0, 2, 1, 3)
    s_idx = jnp.arange(S)

    def block(i):
        t0 = i * QB
        t = t0 + jnp.arange(QB)
        qb = lax.dynamic_slice_in_dim(q, t0, QB, axis=2)
        z = jnp.einsum('bhtd,bhsd->bhts', qb, k).astype(jnp.float32) * scale
        mask = s_idx[None, :] < t[:, None]
        log_rest = jnp.where(mask, jax.nn.log_sigmoid(-z), 0.0)
        rcs = lax.cumsum(log_rest, axis=3, reverse=True)
        between = jnp.pad(rcs[..., 1:], ((0, 0), (0, 0), (0, 0), (0, 1)))
        log_a = jnp.where(mask, jax.nn.log_sigmoid(z) + between, 0.0)
        a = jnp.where(mask, jnp.exp(log_a), 0.0)
        return jnp.einsum('bhts,bhsd->bhtd', a.astype(v.dtype), v)

    o = lax.map(block, jnp.arange(S // QB))
    return o.transpose(1, 0, 3, 2, 4).reshape(B, S, D_SB)


def hgrn2_mixer(q, f_logit, i_val, lb):
    B, S = q.shape[:2]
    H, dk, dv, C = HG_HEADS, HG_KDIM, HG_VDIM, HG_CHUNK
    n_c = S // C
    f32 = f_logit.astype(jnp.float32)
    log_f = jnp.logaddexp(jnp.log(jnp.maximum(lb, LB_FLOOR)), jnp.log1p(-lb) + jax.nn.log_sigmoid(f32))
    key = (1.0 - lb) * jax.nn.sigmoid(-f32)

    def heads(a, d):
        return a.astype(jnp.float32).reshape(B, n_c, C, H, d).transpose(1, 0, 3, 2, 4)

    qc, kc, lfc, vc = heads(q, dk), heads(key, dk), heads(log_f, dk), heads(i_val, dv)
    causal = jnp.asarray(np.tril(np.ones((C, C), dtype=bool)))[None, None, :, :, None]

    def step(state, inp):
        qx, kx, lf, vx = inp
        b = jnp.cumsum(lf, axis=2)
        diff = b[:, :, :, None, :] - b[:, :, None, :, :]
        decay = jnp.where(causal, jnp.exp(jnp.where(causal, diff, 0.0)), 0.0)
        att = jnp.einsum('bhtd,bhsd,bhtsd->bhts', qx, kx, decay)
        o = att @ vx + jnp.einsum('bhtd,bhde->bhte', qx * jnp.exp(b), state)
        b_last = b[:, :, -1:, :]
        state = (jnp.exp(b_last[:, :, 0, :])[..., None] * state
                 + jnp.einsum('bhsd,bhse->bhde', kx * jnp.exp(b_last - b), vx))
        return state, o

    s0 = jnp.zeros((B, H, dk, dv), jnp.float32)
    _, o = lax.scan(step, s0, (qc, kc, lfc, vc))
    return o.transpose(1, 0, 3, 2, 4).reshape(B, S, H, dv)


def setup_inputs(seed: int = 0) -> dict:
    key = jax.random.key(seed)
    ks = jax.random.split(key, 13)

    def nrm(k, shape, scale):
        return jax.random.normal(k, shape, jnp.float32) * scale

    x = nrm(ks[0], (BATCH, SEQ, D_MODEL), 1.0)
    w_in = nrm(ks[1], (DEPTH, D_MODEL, D_IN), D_MODEL ** -0.5)
    cmp_pos = nrm(ks[2], (DEPTH, CMP_BLOCK, HEAD_DIM), 0.1)
    w_ck1 = nrm(ks[3], (DEPTH, CMP_BLOCK * HEAD_DIM, CMP_HIDDEN), (CMP_BLOCK * HEAD_DIM) ** -0.5)
    w_ck2 = nrm(ks[4], (DEPTH, CMP_HIDDEN, HEAD_DIM), CMP_HIDDEN ** -0.5)
    w_cv1 = nrm(ks[5], (DEPTH, CMP_BLOCK * HEAD_DIM, CMP_HIDDEN), (CMP_BLOCK * HEAD_DIM) ** -0.5)
    w_cv2 = nrm(ks[6], (DEPTH, CMP_HIDDEN, HEAD_DIM), CMP_HIDDEN ** -0.5)
    hg_lb = nrm(ks[7], (DEPTH, HG_HEADS * HG_KDIM), 0.5)
    hg_norm_w = 1.0 + nrm(ks[8], (DEPTH, D_HG), 0.05)
    w_out = nrm(ks[9], (DEPTH, D_MIX, D_MODEL), D_MIX ** -0.5 * OUT_INIT_SCALE)
    ln_g = 1.0 + nrm(ks[10], (DEPTH, D_MODEL), 0.05)
    ln_b = nrm(ks[11], (DEPTH, D_MODEL), 0.02)
    rel_bias = nrm(ks[12], (NUM_BUCKETS, NSA_HEADS), 0.5)
    return {'x': x, 'w_in': w_in, 'cmp_pos': cmp_pos, 'w_ck1': w_ck1, 'w_ck2': w_ck2,
            'w_cv1': w_cv1, 'w_cv2': w_cv2, 'hg_lb': hg_lb, 'hg_norm_w': hg_norm_w,
            'w_out': w_out, 'ln_g': ln_g, 'ln_b': ln_b, 'rel_bias': rel_bias}


def reference(x, w_in, cmp_pos, w_ck1, w_ck2, w_cv1, w_cv2, hg_lb, hg_norm_w,
              w_out, ln_g, ln_b, rel_bias):
    lb_w = jax.nn.softmax(hg_lb.astype(jnp.float32), axis=0)
    lb_all = jnp.cumsum(lb_w, axis=0) - lb_w[0]
    offsets = np.cumsum(SPLIT_SIZES)[:-1].tolist()
    for l in range(DEPTH):
        h = x @ w_in[l]
        (nsa_q, nsa_kc, nsa_vc, nsa_ks, nsa_vs, nsa_kw, nsa_vw, nsa_g, nsa_z,
         sb_q, sb_k, sb_v, sb_z, hg_q, hg_f, hg_i, hg_z) = jnp.split(h, offsets, axis=-1)
        o_nsa = nsa_mixer(nsa_q, nsa_kc, nsa_vc, nsa_ks, nsa_vs, nsa_kw, nsa_vw, nsa_g,
                          cmp_pos[l], w_ck1[l], w_ck2[l], w_cv1[l], w_cv2[l], rel_bias)
        o_sb = stick_breaking_mixer(sb_q, sb_k, sb_v)
        o_hg = hgrn2_mixer(hg_q, hg_f, hg_i, lb_all[l])
        o_hg = o_hg * lax.rsqrt(jnp.mean(jnp.square(o_hg), axis=-1, keepdims=True) + RMS_EPS)
        o_hg = (o_hg * hg_norm_w[l].reshape(HG_HEADS, HG_VDIM)).reshape(o_hg.shape[0], o_hg.shape[1], D_HG)
        mixed = jnp.concatenate([o_nsa * jax.nn.silu(nsa_z),
                                 o_sb * jax.nn.silu(sb_z),
                                 o_hg.astype(x.dtype) * jax.nn.silu(hg_z)], axis=-1)
        x = layer_norm(ALPHA * x + (mixed @ w_out[l]).astype(x.dtype), ln_g[l], ln_b[l])
    return x
```

```python
import math
import numpy as np
from contextlib import ExitStack
import concourse.bass as bass
import concourse.mybir as mybir
from concourse.bass_utils import run_bass_kernel_spmd

F32 = mybir.dt.float32
BF16 = mybir.dt.bfloat16
AF = mybir.ActivationFunctionType
ALU = mybir.AluOpType
AX = mybir.AxisListType

S = 4096
DM = 1024
DEPTH = 2
D_IN = 4114
NT = S // 128
ALPHA = (2 * DEPTH) ** 0.25
SC = 0.125
BIG = 16384.0

NSLOT = 8
DMA_QUEUES = ('sync', 'gpsimd')
COMPUTE = ('tensor', 'vector', 'scalar', 'gpsimd')
ALLENG = ('sync', 'tensor', 'vector', 'scalar', 'gpsimd')


class Prog:
    def __init__(self, nc, same_engine_sync=True):
        self.nc = nc
        self.ops = []
        self.stack = ExitStack()
        self.same_engine_sync = same_engine_sync
        self._n = 0
        self.arena = None
        self.apos = 0

    def sb(self, shape, dtype, name=None):
        self._n += 1
        return self.stack.enter_context(self.nc.sbuf_tensor("S_" + (name or f"sb{self._n}"), list(shape), dtype))

    def ps(self, shape, dtype, name=None):
        self._n += 1
        return self.stack.enter_context(self.nc.psum_tensor("P_" + (name or f"ps{self._n}"), list(shape), dtype))

    def make_arena(self, nf32):
        self.arena = self.sb([128, nf32], F32, "arena")
        self.asize = nf32

    def alloc(self, n, dtype):
        w = n if dtype == F32 else (n + 1) // 2
        w = (w + 7) // 8 * 8
        assert self.apos + w <= self.asize, (self.apos, w, self.asize)
        a = self.arena[:, self.apos:self.apos + w]
        self.apos += w
        if dtype == F32:
            return a[:, 0:n]
        return a.bitcast(BF16)[:, 0:n]

    def op(self, eng, fn, r=(), w=()):
        self.ops.append((eng, fn, tuple(r), tuple(w), False))

    def dma(self, queue, out, in_, r=(), w=(), **kw):
        if out.dtype != in_.dtype:
            assert queue == 'gpsimd'
            kw.setdefault('max_dma_last_dim', 4096)
        self.ops.append((queue, (out, in_, kw), tuple(r), tuple(w), True))

    def barrier(self):
        self.ops.append(('BARRIER', None, (), (), False))
        self.apos = 0

    def emit(self):
        nc = self.nc
        st = self.stack
        sem_c = {e: st.enter_context(nc.semaphore(f"s_{e}")) for e in COMPUTE}
        sem_d = {q: [st.enter_context(nc.semaphore(f"d_{q}{i}")) for i in range(NSLOT)] for q in DMA_QUEUES}
        cnt_c = {e: 0 for e in COMPUTE}
        cnt_d = {q: 0 for q in DMA_QUEUES}
        last_w = {}
        readers = {}
        token = []
        streams = {e: [] for e in ALLENG}
        waited = {e: {} for e in ALLENG}
        semobj = {}
        for e in COMPUTE:
            semobj[('c', e)] = sem_c[e]
        for q in DMA_QUEUES:
            for i in range(NSLOT):
                semobj[('d', q, i)] = sem_d[q][i]

        def need(eng, tok):
            sk, v = tok
            if v <= 0 or waited[eng].get(sk, 0) >= v:
                return
            waited[eng][sk] = v
            streams[eng].append(('wait', sk, v))

        def all_tokens():
            toks = []
            for q in DMA_QUEUES:
                n = cnt_d[q]
                for slot in range(NSLOT):
                    if n > slot:
                        m_last = ((n - 1 - slot) // NSLOT) * NSLOT + slot
                        toks.append((('d', q, slot), 16 * (m_last // NSLOT + 1)))
            for e in COMPUTE:
                if cnt_c[e]:
                    toks.append((('c', e), cnt_c[e]))
            return toks

        for i, (eng, fn, rs, ws, is_dma) in enumerate(self.ops):
            if eng == 'BARRIER':
                toks = all_tokens()
                for e in ALLENG:
                    for tk in toks:
                        need(e, tk)
                last_w.clear()
                readers.clear()
                token.append(None)
                continue
            deps = set()
            for k in rs:
                if k in last_w:
                    deps.add(last_w[k])
            for k in ws:
                if k in last_w:
                    deps.add(last_w[k])
                for j in readers.get(k, ()):
                    deps.add(j)
            for j in sorted(deps):
                je, _, _, _, jdma = self.ops[j]
                if (not jdma) and je == eng and not is_dma:
                    if eng == 'tensor' or not self.same_engine_sync:
                        continue
                need(eng, token[j])
            if is_dma:
                m = cnt_d[eng]
                cnt_d[eng] += 1
                slot = m % NSLOT
                sk = ('d', eng, slot)
                if m >= NSLOT:
                    need(eng, (sk, 16 * (m // NSLOT)))
                tok = (sk, 16 * (m // NSLOT + 1))
                streams[eng].append(('dma', fn, sk))
            else:
                cnt_c[eng] += 1
                sk = ('c', eng)
                tok = (sk, cnt_c[eng])
                streams[eng].append(('op', fn, sk))
            token.append(tok)
            for k in ws:
                last_w[k] = i
                readers[k] = []
            for k in rs:
                if k not in ws:
                    readers.setdefault(k, []).append(i)
        for tk in all_tokens():
            need('sync', tk)

        def run_stream(name, engobj):
            for item in streams[name]:
                if item[0] == 'wait':
                    engobj.wait_ge(semobj[item[1]], item[2])
                elif item[0] == 'dma':
                    out, in_, kw = item[1]
                    engobj.dma_start(out=out, in_=in_, **kw).then_inc(semobj[item[2]], 16)
                else:
                    item[1](engobj).then_inc(semobj[item[2]], 1)

        with nc.Block() as block:
            @block.sync
            def _(e):
                run_stream('sync', e)

            @block.tensor
            def _(e):
                run_stream('tensor', e)

            @block.vector
            def _(e):
                run_stream('vector', e)

            @block.scalar
            def _(e):
                run_stream('scalar', e)

            @block.gpsimd
            def _(e):
                run_stream('gpsimd', e)
        self.stack.close()
        self.counts = (dict(cnt_c), dict(cnt_d))


def MM(P, out, lhsT, rhs, start, stop, r, w):
    P.op('tensor', lambda e: e.matmul(out, lhsT, rhs, start=start, stop=stop), r, w)


def TR(P, out, in_, ident, r, w):
    P.op('tensor', lambda e: e.transpose(out, in_, ident), r, w)


def ACT(P, out, in_, func, r, w, bias=None, scale=None, accum=None):
    kw = {}
    if bias is not None:
        kw['bias'] = bias
    if scale is not None:
        kw['scale'] = scale
    if accum is not None:
        kw['accum_out'] = accum
    P.op('scalar', lambda e: e.activation(out, in_, func, **kw), r, w)


def TT(P, eng, out, in0, in1, op, r, w):
    P.op(eng, lambda e: e.tensor_tensor(out=out, in0=in0, in1=in1, op=op), r, w)


def TS(P, eng, out, in0, s1, s2, op0, op1, r, w):
    if op1 is None:
        P.op(eng, lambda e: e.tensor_scalar(out, in0, s1, None, op0), r, w)
    else:
        P.op(eng, lambda e: e.tensor_scalar(out, in0, s1, s2, op0, op1), r, w)


def STT(P, out, in0, scalar, in1, op0, op1, r, w):
    P.op('vector', lambda e: e.scalar_tensor_tensor(out=out, in0=in0, scalar=scalar, in1=in1, op0=op0, op1=op1), r, w)


def CP(P, eng, out, in_, r, w):
    if eng == 'scalar':
        P.op('scalar', lambda e: e.copy(out, in_), r, w)
    else:
        P.op(eng, lambda e: e.tensor_copy(out=out, in_=in_), r, w)


def MEMSET(P, eng, ap, val, w):
    P.op(eng, lambda e: e.memset(ap, val), (), w)


def t5_bucket_np(rel):
    n = np.maximum(rel, 0)
    nf = np.maximum(n, 1).astype(np.float32)
    large = 16 + (np.log(nf / np.float32(16)) / np.float32(math.log(8.0)) * np.float32(16)).astype(np.int32)
    large = np.clip(large, 0, 31)
    return np.where(n < 16, n, large).astype(np.int64)


def host_consts():
    c = {}
    c['ident'] = np.eye(128, dtype=np.float32)
    sl = np.arange(128)[:, None]
    tl = np.arange(128)[None, :]
    c['m0'] = (tl >= sl).astype(np.float32)
    c['m4'] = (sl > tl).astype(np.float32)
    c['mstrict'] = (tl < sl).astype(np.float32)
    ch_s = sl // 64
    ch_t = tl // 64
    same = (ch_s == ch_t)
    mid_t = ch_t * 64 + 31
    c['hg_mb'] = (same & (sl <= tl)).astype(np.float32)
    c['hg_md'] = (same * ((sl <= tl).astype(np.float32) - (sl <= mid_t).astype(np.float32))).astype(np.float32)
    c['hg_mk'] = (same & (sl > tl)).astype(np.float32)
    ci = np.zeros((128, 2), np.float32)
    ci[:64, 0] = 1
    ci[64:, 1] = 1
    c['hg_ci'] = ci
    n = np.arange(256)
    j = np.arange(64)
    cs = n[:, None] * 16
    ov = ((cs < j[None, :] * 64 + 64) & (cs + 32 > j[None, :] * 64)).astype(np.float32)
    ov[255] = 0
    c['ov'] = np.ascontiguousarray(ov.reshape(2, 128, 64).transpose(1, 0, 2))
    t = np.arange(S)
    cur = t // 64
    forced = (j[None, :] == 0) | (j[None, :] == cur[:, None]) | (j[None, :] == cur[:, None] - 1)
    valid = j[None, :] * 64 <= t[:, None]
    fc = np.where(valid, 1000.0 * forced, -1e30).astype(np.float32)
    c['fc'] = np.ascontiguousarray(fc.reshape(32, 128, 64).transpose(1, 0, 2))
    ew = np.zeros((64, S), np.float32)
    ew[np.arange(S) // 64, np.arange(S)] = BIG
    c['ew'] = ew
    rv = np.ones((128, 1), np.float32)
    rv[:31] = 0
    c['rv0'] = rv
    return c


def host_gathers(rel_bias):
    g = {}
    tl = np.arange(128)[:, None]
    m = np.arange(511)[None, :]
    rel = tl - 16 * (m - 255) - 31
    wc = rel_bias[t5_bucket_np(rel)]
    wc = np.where((rel >= 0)[:, :, None], wc, np.float32(-30000.0)).astype(np.float32)
    g['wc'] = np.ascontiguousarray(wc.transpose(0, 2, 1))
    sl = np.arange(128)[:, None]
    t2 = np.arange(128)[None, :]
    tz0 = rel_bias[t5_bucket_np(t2 - sl)]
    tz1 = rel_bias[t5_bucket_np(128 + t2 - sl)]
    g['tz'] = np.ascontiguousarray(np.stack([tz0, tz1], 0).transpose(1, 3, 0, 2)).astype(np.float32)
    g['b31'] = np.ascontiguousarray(np.broadcast_to(rel_bias[31][None, :], (128, 6))).astype(np.float32)
    return g


CONST_SHAPES = {
    'ident': [128, 128], 'm0': [128, 128], 'm4': [128, 128], 'mstrict': [128, 128],
    'hg_mb': [128, 128], 'hg_md': [128, 128], 'hg_mk': [128, 128], 'hg_ci': [128, 2],
    'ov': [128, 2, 64], 'fc': [128, 32, 64], 'ew': [64, S], 'rv0': [128, 1],
    'wc': [128, 6, 511], 'tz': [128, 6, 2, 128], 'b31': [128, 6],
}
IN_SHAPES = {
    'x': [S, DM], 'w_in': [DEPTH, DM, D_IN], 'cmp_posT': [DEPTH, 64, 32],
    'w_ck1': [DEPTH, 2048, 128], 'w_ck2': [DEPTH, 128, 64], 'w_cv1': [DEPTH, 2048, 128], 'w_cv2': [DEPTH, 128, 64],
    'hg_lb_b': [128, DEPTH, 384], 'hg_nw_b': [128, DEPTH, 384], 'w_out': [DEPTH, DM, DM],
    'lng_b': [128, DEPTH, DM], 'lnb_b': [128, DEPTH, DM],
}
FEAT_CHUNK_SRC = [0, 128, 256, 384, 512, 640, 896, 1554, 1682, 1810, 1938]
F_Q, F_KC, F_VC, F_KS, F_KW, F_SBQ, F_SBK = 0, 384, 512, 640, 768, 896, 1152
NFEAT = 1408
TOK_PIECES = [(768, 128, 0), (1024, 128, 128), (1152, 402, 256), (2066, 512, 658), (2578, 512, 1170),
              (3090, 512, 1682), (3602, 512, 2194)]
T_VS, T_VW, T_G, T_Z, T_SBV, T_SBZ, T_HGQ, T_HGF, T_HGI, T_HGZ = 0, 128, 256, 274, 658, 914, 1170, 1554, 1938, 2322
NTOK = 2706


class Ctx:
    pass


def phase_proj(P, C, l, xsrc):
    W = P.alloc(8 * D_IN, BF16).rearrange("p (k c) -> p k c", k=8)
    xT = P.alloc(8 * S, BF16).rearrange("p (k t) -> p k t", k=8)
    wsrc = C.d['w_in'][l].rearrange("(k p) c -> p k c", p=128)
    for k in range(8):
        for c0 in range(0, D_IN, 2048):
            c1 = min(D_IN, c0 + 2048)
            P.dma('gpsimd', W[:, k, c0:c1], wsrc[:, k, c0:c1], w=[('W', k)])
    xb = [P.alloc(DM, BF16) for _ in range(2)]
    fst = [P.alloc(512, F32) for _ in range(4)]
    for tt in range(NT):
        b = tt % 2
        P.dma('gpsimd', xb[b], xsrc[tt * 128:(tt + 1) * 128, :], w=[('xb', b)])
        pb = C.psb[6 + b]
        for k in range(8):
            TR(P, pb[:, k * 128:(k + 1) * 128], xb[b][:, k * 128:(k + 1) * 128], C.identb[:], [('xb', b)], [('ps', 6 + b)])
        CP(P, 'vector', xT[:, 0:4, tt * 128:(tt + 1) * 128], pb[:, 0:512].rearrange("p (k t) -> p k t", k=4), [('ps', 6 + b)], [('xT', tt)])
        CP(P, 'scalar', xT[:, 4:8, tt * 128:(tt + 1) * 128], pb[:, 512:1024].rearrange("p (k t) -> p k t", k=4), [('ps', 6 + b)], [('xT', tt)])
    wkeys = [('W', k) for k in range(8)]
    nps = 0
    ne = 0
    for tt in range(NT):
        for pi, (c0, n, d0) in enumerate(TOK_PIECES):
            bank = nps % 6
            nps += 1
            for k in range(8):
                MM(P, C.psf[bank][:, 0:n], xT[:, k, tt * 128:(tt + 1) * 128], W[:, k, c0:c0 + n], k == 0, k == 7,
                   [('xT', tt)] + wkeys, [('ps', bank)])
            b = ne % 4
            ne += 1
            CP(P, 'vector' if ne % 2 == 0 else 'scalar', fst[b][:, 0:n], C.psf[bank][:, 0:n], [('ps', bank)], [('fst', b)])
            P.dma('sync', C.htok[tt * 128:(tt + 1) * 128, d0:d0 + n], fst[b][:, 0:n], r=[('fst', b)], w=['htok'])
    for t0 in range(0, S, 512):
        xk = [('xT', tt) for tt in range(t0 // 128, t0 // 128 + 4)]
        for ci, src in enumerate(FEAT_CHUNK_SRC):
            bank = nps % 6
            nps += 1
            for k in range(8):
                MM(P, C.psf[bank][:, :], W[:, k, src:src + 128], xT[:, k, t0:t0 + 512], k == 0, k == 7, xk + wkeys, [('ps', bank)])
            b = ne % 4
            ne += 1
            CP(P, 'vector' if ne % 2 == 0 else 'scalar', fst[b], C.psf[bank][:, :], [('ps', bank)], [('fst', b)])
            P.dma('sync', C.hfeat[ci * 128:(ci + 1) * 128, t0:t0 + 512], fst[b], r=[('fst', b)], w=['hfeat'])
    P.barrier()


def gelu_tanh(P, out, u, tmp, key):
    TT(P, 'vector', tmp, u, u, ALU.mult, [key + 'u'], [key + 't'])
    TS(P, 'vector', tmp, tmp, 0.044715, 1.0, ALU.mult, ALU.add, [key + 't'], [key + 't'])
    TT(P, 'vector', tmp, tmp, u, ALU.mult, [key + 't', key + 'u'], [key + 't'])
    ACT(P, tmp, tmp, AF.Tanh, [key + 't'], [key + 't'], scale=0.7978845608028654)
    TS(P, 'vector', tmp, tmp, 1.0, 0.5, ALU.add, ALU.mult, [key + 't'], [key + 't'])
    TT(P, 'vector', out, tmp, u, ALU.mult, [key + 't', key + 'u'], [key + 'o'])


def phase_cmp_mlp(P, C, l):
    src = P.alloc(2 * S, F32).rearrange("p (a t) -> p a t", a=2)
    P.dma('sync', src[:, 0, :], C.hfeat[F_KC:F_KC + 128, :], w=['src0'])
    P.dma('sync', src[:, 1, :], C.hfeat[F_VC:F_VC + 128, :], w=['src1'])
    w1 = P.alloc(2 * 32 * 128, F32).rearrange("p (a q h) -> p a q h", a=2, q=32)
    w2 = P.alloc(2 * 64, F32).rearrange("p (a h) -> p a h", a=2)
    cpT = P.alloc(32, F32)
    for a, (n1, n2) in enumerate((('w_ck1', 'w_ck2'), ('w_cv1', 'w_cv2'))):
        for half in range(2):
            for q0 in range(0, 32, 8):
                P.dma('sync', w1[half * 64:(half + 1) * 64, a, q0:q0 + 8], C.d[n1][l][q0 * 64:(q0 + 8) * 64, :].rearrange("(q d) h -> d q h", d=64), w=['w1'])
        P.dma('sync', w2[:, a], C.d[n2][l], w=['w2'])
    for half in range(2):
        P.dma('sync', cpT[half * 64:(half + 1) * 64, :], C.d['cmp_posT'][l], w=['cpT'])
    u = P.alloc(256, F32)
    tmp = P.alloc(256, F32)
    g1 = P.alloc(256, F32)
    c1 = P.alloc(1, F32)
    ev = P.alloc(256, F32)
    for a in range(2):
        for g in range(2):
            pr = slice(g * 64, (g + 1) * 64)
            key = f"cm{a}{g}"
            for q in range(32):
                MM(P, C.psf[4 * g + 0][:, 0:255], w1[pr, a, q, :], src[pr, a, q:q + 16 * 254 + 1:16], q == 0, q == 31,
                   ['w1', f'src{a}'], [('ps', 4 * g + 0)])
            for q in range(32):
                MM(P, C.psf[4 * g + 1][:, 0:1], w1[pr, a, q, :], cpT[pr, q:q + 1], q == 0, q == 31, ['w1', 'cpT'], [('ps', 4 * g + 1)])
            CP(P, 'vector', c1, C.psf[4 * g + 1][:, 0:1], [('ps', 4 * g + 1)], [key + 'c1'])
            ACT(P, u[:, 0:255], C.psf[4 * g + 0][:, 0:255], AF.Identity, [('ps', 4 * g + 0), key + 'c1'], [key + 'u'], bias=c1)
            gelu_tanh(P, g1[:, 0:255], u[:, 0:255], tmp[:, 0:255], key)
            if a == 0:
                MM(P, C.psf[4 * g + 2][0:64, 0:255], w2[:, a, :], g1[:, 0:255], True, True, ['w2', key + 'o'], [('ps', 4 * g + 2)])
                CP(P, 'vector', C.kcT[0:64, g, 0:255], C.psf[4 * g + 2][0:64, 0:255], [('ps', 4 * g + 2)], ['kcT'])
            else:
                for nt in range(2):
                    n = 128 if nt == 0 else 127
                    MM(P, C.psf[4 * g + 3][0:n, nt * 64:(nt + 1) * 64], g1[:, nt * 128:nt * 128 + n], w2[:, a, :], True, True,
                       ['w2', key + 'o'], [('ps', 4 * g + 3)])
                    CP(P, 'vector', C.vc[0:n, g, nt, :], C.psf[4 * g + 3][0:n, nt * 64:(nt + 1) * 64], [('ps', 4 * g + 3)], ['vc'])
    P.barrier()


def phase_cmp_att(P, C, l):
    wc = P.alloc(6 * 511, F32).rearrange("p (h m) -> p h m", h=6)
    P.dma('sync', wc, C.d['wc'], w=['wc'])
    fc = P.alloc(32 * 64, F32).rearrange("p (a j) -> p a j", a=32)
    P.dma('sync', fc, C.d['fc'], w=['fc'])
    ov = P.alloc(128, F32).rearrange("p (a j) -> p a j", a=2)
    P.dma('sync', ov, C.d['ov'], w=['ov'])
    rv0 = P.alloc(1, F32)
    P.dma('sync', rv0, C.d['rv0'], w=['rv0'])
    qT = [P.alloc(6 * 128, F32).rearrange("p (h t) -> p h t", h=6) for _ in range(2)]
    gl = [P.alloc(18, F32) for _ in range(2)]
    on = [P.alloc(384, F32) for _ in range(2)]
    sbuf = [P.alloc(256, F32) for _ in range(2)]
    ebuf = [P.alloc(256, F32) for _ in range(2)]
    pbuf = [P.alloc(256, F32) for _ in range(2)]
    pT = [P.alloc(256, F32).rearrange("p (a t) -> p a t", a=2) for _ in range(2)]
    pg = P.alloc(256, F32)
    pgT = P.alloc(256, F32).rearrange("p (a t) -> p a t", a=2)
    sm = [P.alloc(8, F32) for _ in range(2)]
    sc = P.alloc(64, F32)
    sc2 = P.alloc(64, F32)
    m8 = P.alloc(16, F32)
    nm = P.alloc(64, F32)
    qsrc = C.hfeat[F_Q:F_Q + 384, :].rearrange("(h d) t -> d h t", d=64)
    hh = 0
    for tb in range(NT):
        b = tb % 2
        P.dma('sync', qT[b][0:64], qsrc[:, :, tb * 128:(tb + 1) * 128], w=[('qT', b)])
        P.dma('sync', gl[b], C.htok[tb * 128:(tb + 1) * 128, T_G:T_G + 18], w=[('gl', b)])
        ACT(P, gl[b], gl[b], AF.Sigmoid, [('gl', b)], [('gl', b)])
        for g in range(2):
            for hp in range(3):
                h = g * 3 + hp
                hb = hh % 2
                hh += 1
                bank = hb
                MM(P, C.psf[bank][:, 0:256], qT[b][0:64, h, :], C.kcT[0:64, g, :], True, True, [('qT', b), 'kcT'], [('ps', bank)])
                STT(P, sbuf[hb], C.psf[bank][:, 0:256], SC, wc[:, h, 255 - 8 * tb:255 - 8 * tb + 256], ALU.mult, ALU.add,
                    [('ps', bank), 'wc'], [('sbuf', hb)])
                P.op('vector', (lambda o, i: lambda e: e.tensor_reduce(out=o, in_=i, axis=AX.X, op=ALU.max, negate=True))(sm[hb][:, 0:1], sbuf[hb]),
                     [('sbuf', hb)], [('sm0', hb)])
                ACT(P, ebuf[hb], sbuf[hb], AF.Exp, [('sbuf', hb), ('sm0', hb)], [('ebuf', hb)], bias=sm[hb][:, 0:1])
                P.op('vector', (lambda o, i: lambda e: e.tensor_reduce(out=o, in_=i, axis=AX.X, op=ALU.add))(sm[hb][:, 1:2], ebuf[hb]),
                     [('ebuf', hb)], [('sm1', hb)])
                P.op('vector', (lambda o, i: lambda e: e.reciprocal(o, i))(sm[hb][:, 2:3], sm[hb][:, 1:2]), [('sm1', hb)], [('sm2', hb)])
                if tb == 0:
                    TT(P, 'vector', sm[hb][:, 2:3], sm[hb][:, 2:3], rv0, ALU.mult, [('sm2', hb), 'rv0'], [('sm2', hb)])
                TS(P, 'vector', pbuf[hb], ebuf[hb], sm[hb][:, 2:3], None, ALU.mult, None, [('ebuf', hb), ('sm2', hb)], [('pbuf', hb)])
                if hp == 0:
                    CP(P, 'gpsimd', pg, pbuf[hb], [('pbuf', hb)], ['pg'])
                else:
                    TT(P, 'gpsimd', pg, pg, pbuf[hb], ALU.add, ['pg', ('pbuf', hb)], ['pg'])
                tbank = 2 + hb
                for a in range(2):
                    TR(P, C.psf[tbank][:, a * 128:(a + 1) * 128], pbuf[hb][:, a * 128:(a + 1) * 128], C.identf[:], [('pbuf', hb)], [('ps', tbank)])
                CP(P, 'scalar', pT[hb], C.psf[tbank][:, 0:256].rearrange("p (a t) -> p a t", a=2), [('ps', tbank)], [('pT', hb)])
                obank = 4 + hb
                for a in range(2):
                    MM(P, C.psf[obank][:, 0:64], pT[hb][:, a, :], C.vc[:, g, a, :], a == 0, a == 1, [('pT', hb), 'vc'], [('ps', obank)])
                TS(P, 'vector', on[b][:, h * 64:(h + 1) * 64], C.psf[obank][:, 0:64], gl[b][:, 3 * h:3 * h + 1], None, ALU.mult, None,
                   [('ps', obank), ('gl', b)], [('on', b, h)])
            for a in range(2):
                TR(P, C.psf[6][:, a * 128:(a + 1) * 128], pg[:, a * 128:(a + 1) * 128], C.identf[:], ['pg'], [('ps', 6)])
            CP(P, 'scalar', pgT, C.psf[6][:, 0:256].rearrange("p (a t) -> p a t", a=2), [('ps', 6)], ['pgT'])
            for a in range(2):
                MM(P, C.psf[7][:, 0:64], pgT[:, a, :], ov[:, a, :], a == 0, a == 1, ['pgT', 'ov'], [('ps', 7)])
            TT(P, 'vector', sc, C.psf[7][:, 0:64], fc[:, tb, :], ALU.add, [('ps', 7), 'fc'], ['sc'])
            P.op('vector', (lambda o, i: lambda e: e.max(out=o, in_=i))(m8[:, 0:8], sc), ['sc'], ['m8a'])
            P.op('vector', (lambda o, r_, v: lambda e: e.match_replace(out=o, in_to_replace=r_, in_values=v, imm_value=-3.0e38))(sc2, m8[:, 0:8], sc),
                 ['sc', 'm8a'], ['sc2'])
            P.op('vector', (lambda o, i: lambda e: e.max(out=o, in_=i))(m8[:, 8:16], sc2), ['sc2'], ['m8b'])
            TS(P, 'vector', nm, sc, m8[:, 15:16], 1.0, ALU.is_ge, ALU.subtract, ['sc', 'm8b'], ['nm'])
            TR(P, C.psf[7][0:64, 128:256], nm, C.identf[:], ['nm'], [('ps', 7)])
            CP(P, 'vector', C.NM[g][0:64, tb * 128:(tb + 1) * 128], C.psf[7][0:64, 128:256], [('ps', 7)], [('NM', g)])
        P.dma('gpsimd', C.onsa[tb * 128:(tb + 1) * 128, :], on[b], r=[('on', b, h) for h in range(6)], w=['onsa'])
    P.barrier()


def phase_slcwin(P, C, l):
    tz = P.alloc(6 * 2 * 128, F32).rearrange("p (h a t) -> p h a t", h=6, a=2)
    P.dma('sync', tz, C.d['tz'], w=['tz'])
    b31 = P.alloc(6, F32)
    P.dma('sync', b31, C.d['b31'], w=['b31'])
    TS(P, 'vector', b31, b31, -1.0, None, ALU.mult, None, ['b31'], ['b31'])
    cc = P.alloc(6 * 2 * 128, BF16).rearrange("p (h a t) -> p h a t", h=6, a=2)
    for h in range(6):
        for a in range(2):
            ACT(P, tz[:, h, a, :], tz[:, h, a, :], AF.Exp, ['tz', 'b31'], ['tz'], bias=b31[:, h:h + 1])
    for h in range(6):
        TT(P, 'vector', cc[:, h, 0, :], tz[:, h, 0, :], C.m0[:], ALU.mult, ['tz', 'm0'], ['cc'])
        CP(P, 'vector', cc[:, h, 1, :], tz[:, h, 1, :], ['tz'], ['cc'])
    ew = P.alloc(S, BF16)
    P.dma('gpsimd', ew[0:64], C.d['ew'], w=['ew'])
    ksT = P.alloc(S, BF16)
    kwT = P.alloc(S, BF16)
    vs = P.alloc(32 * 65, BF16).rearrange("p (a e) -> p a e", a=32)
    vw = P.alloc(32 * 65, BF16).rearrange("p (a e) -> p a e", a=32)
    qT = [P.alloc(S, BF16) for _ in range(2)]
    gl = P.alloc(32 * 18, F32).rearrange("p (a e) -> p a e", a=32)
    for a_ in range(32):
        P.dma('sync', gl[:, a_, :], C.htok[a_ * 128:(a_ + 1) * 128, T_G:T_G + 18], w=['gl'])
    ACT(P, gl, gl, AF.Sigmoid, ['gl'], ['gl'])
    PTs = P.alloc(32 * 512, BF16).rearrange("p (a t) -> p a t", a=32)
    PTw = P.alloc(8 * 512, BF16).rearrange("p (a t) -> p a t", a=8)
    on = [P.alloc(384, F32) for _ in range(2)]
    onl = [P.alloc(384, F32) for _ in range(2)]
    sm = [P.alloc(4, F32) for _ in range(2)]
    nps = 0
    nob = 0
    for h in range(6):
        g = h // 3
        hb = h % 2
        if h % 3 == 0:
            P.dma('gpsimd', ksT[0:64], C.hfeat[F_KS + g * 64:F_KS + (g + 1) * 64, :], w=['ksT'])
            P.dma('gpsimd', kwT[0:64], C.hfeat[F_KW + g * 64:F_KW + (g + 1) * 64, :], w=['kwT'])
            MEMSET(P, 'vector', vs[:, :, 64:65], 1.0, ['vs'])
            MEMSET(P, 'vector', vw[:, :, 64:65], 1.0, ['vw'])
            for a_ in range(0, 32, 8):
                P.dma('gpsimd', vs[:, a_:a_ + 8, 0:64], C.htok[a_ * 128:(a_ + 8) * 128, T_VS + g * 64:T_VS + (g + 1) * 64].rearrange("(a p) e -> p a e", p=128), r=['vs'], w=['vs'])
                P.dma('gpsimd', vw[:, a_:a_ + 8, 0:64], C.htok[a_ * 128:(a_ + 8) * 128, T_VW + g * 64:T_VW + (g + 1) * 64].rearrange("(a p) e -> p a e", p=128), r=['vw'], w=['vw'])
        P.dma('gpsimd', qT[hb][0:64], C.hfeat[F_Q + h * 64:F_Q + (h + 1) * 64, :], w=[('qT', hb)])
        for qt in range(8):
            t0 = qt * 512
            for kt in range(4 * qt + 4):
                bank = nps % 4
                nps += 1
                c0 = max(0, kt - 4 * qt) * 128
                MM(P, C.psf[bank][:, c0:512], ksT[0:64, kt * 128:(kt + 1) * 128], qT[hb][0:64, t0 + c0:t0 + 512], True, False,
                   ['ksT', ('qT', hb)], [('ps', bank)])
                MM(P, C.psf[bank][:, c0:512], ew[0:64, kt * 128:(kt + 1) * 128], C.NM[g][0:64, t0 + c0:t0 + 512], False, True,
                   ['ew', ('NM', g)], [('ps', bank)])
                ACT(P, PTs[:, kt, c0:512], C.psf[bank][:, c0:512], AF.Exp, [('ps', bank)], [('PTs', kt)], scale=SC)
                for tq in range(4):
                    dl = 4 * qt + tq - kt
                    if dl in (0, 1):
                        TT(P, 'gpsimd', PTs[:, kt, tq * 128:(tq + 1) * 128], PTs[:, kt, tq * 128:(tq + 1) * 128], cc[:, h, dl, :], ALU.mult,
                           [('PTs', kt), 'cc'], [('PTs', kt)])
            kts = [kt for kt in range(4 * qt - 4, 4 * qt + 4) if kt >= 0]
            for kt in kts:
                j = kt - (4 * qt - 4)
                bank = nps % 4
                nps += 1
                lo = max(0, kt - 4 * qt)
                hi = min(3, kt + 4 - 4 * qt)
                c0, c1 = lo * 128, (hi + 1) * 128
                MM(P, C.psf[bank][:, c0:c1], kwT[0:64, kt * 128:(kt + 1) * 128], qT[hb][0:64, t0 + c0:t0 + c1], True, True,
                   ['kwT', ('qT', hb)], [('ps', bank)])
                ACT(P, PTw[:, j, c0:c1], C.psf[bank][:, c0:c1], AF.Exp, [('ps', bank)], [('PTw', j)], scale=SC)
                for tq in range(lo, hi + 1):
                    dl = 4 * qt + tq - kt
                    if dl in (0, 1):
                        msk = cc[:, h, dl, :]
                    elif dl == 4:
                        msk = C.m4b[:]
                    else:
                        continue
                    TT(P, 'gpsimd', PTw[:, j, tq * 128:(tq + 1) * 128], PTw[:, j, tq * 128:(tq + 1) * 128], msk, ALU.mult,
                       [('PTw', j), 'cc', 'm4b'], [('PTw', j)])
            for tq in range(4):
                tb = 4 * qt + tq
                ob = nob % 2
                nob += 1
                bs, bw = 4 + ob, 6 + ob
                nk = tb + 1
                for kt in range(nk):
                    MM(P, C.psf[bs][:, 0:65], PTs[:, kt, tq * 128:(tq + 1) * 128], vs[:, kt, :], kt == 0, kt == nk - 1,
                       [('PTs', kt), 'vs'], [('ps', bs)])
                wk = [kt for kt in range(tb - 4, tb + 1) if kt >= 0]
                for i, kt in enumerate(wk):
                    j = kt - (4 * qt - 4)
                    MM(P, C.psf[bw][:, 0:65], PTw[:, j, tq * 128:(tq + 1) * 128], vw[:, kt, :], i == 0, i == len(wk) - 1,
                       [('PTw', j), 'vw'], [('ps', bw)])
                P.op('vector', (lambda o, i: lambda e: e.reciprocal(o, i))(sm[ob][:, 0:1], C.psf[bs][:, 64:65]), [('ps', bs)], [('sm', ob)])
                P.op('vector', (lambda o, i: lambda e: e.reciprocal(o, i))(sm[ob][:, 1:2], C.psf[bw][:, 64:65]), [('ps', bw)], [('sm', ob)])
                TT(P, 'vector', sm[ob][:, 0:1], sm[ob][:, 0:1], gl[:, tb, 3 * h + 1:3 * h + 2], ALU.mult, [('sm', ob), 'gl'], [('sm', ob)])
                TT(P, 'vector', sm[ob][:, 1:2], sm[ob][:, 1:2], gl[:, tb, 3 * h + 2:3 * h + 3], ALU.mult, [('sm', ob), 'gl'], [('sm', ob)])
                P.dma('sync', onl[ob][:, 0:64], C.onsa[tb * 128:(tb + 1) * 128, h * 64:(h + 1) * 64], r=[('onsa', tb)], w=[('onl', ob)])
                STT(P, on[ob][:, 0:64], C.psf[bs][:, 0:64], sm[ob][:, 0:1], onl[ob][:, 0:64], ALU.mult, ALU.add,
                    [('ps', bs), ('sm', ob), ('onl', ob)], [('on', ob)])
                STT(P, on[ob][:, 0:64], C.psf[bw][:, 0:64], sm[ob][:, 1:2], on[ob][:, 0:64], ALU.mult, ALU.add,
                    [('ps', bw), ('sm', ob), ('on', ob)], [('on', ob)])
                P.dma('sync', C.onsa[tb * 128:(tb + 1) * 128, h * 64:(h + 1) * 64], on[ob][:, 0:64], r=[('on', ob)], w=[('onsa', tb)])
    P.barrier()


def phase_sb(P, C, l):
    kT = P.alloc(4 * S, BF16).rearrange("p (h t) -> p h t", h=4)
    qT = P.alloc(4 * S, BF16).rearrange("p (h t) -> p h t", h=4)
    v = P.alloc(32 * 256, BF16).rearrange("p (a e) -> p a e", a=32)
    for h in range(4):
        P.dma('gpsimd', kT[0:64, h, :], C.hfeat[F_SBK + h * 64:F_SBK + (h + 1) * 64, :], w=['kT'])
        P.dma('gpsimd', qT[0:64, h, :], C.hfeat[F_SBQ + h * 64:F_SBQ + (h + 1) * 64, :], w=['qT'])
    for a_ in range(0, 32, 8):
        P.dma('gpsimd', v[:, a_:a_ + 8, :], C.htok[a_ * 128:(a_ + 8) * 128, T_SBV:T_SBV + 256].rearrange("(a p) e -> p a e", p=128), w=['v'])
    cp = P.alloc(S + 8, F32)
    zeros = P.alloc(512, F32)
    MEMSET(P, 'vector', zeros, 0.0, ['zeros'])
    nms = P.alloc(128, F32)
    TS(P, 'vector', nms, C.mstrict[:], -1.0, 1.0, ALU.mult, ALU.add, ['mstrict'], ['nms'])
    bt = [P.alloc(512, F32) for _ in range(2)]
    om = [P.alloc(520, F32) for _ in range(2)]
    at = [P.alloc(512, BF16) for _ in range(2)]
    aT = [P.alloc(512, BF16).rearrange("p (a t) -> p a t", a=4) for _ in range(2)]
    osb = [P.alloc(256, F32) for _ in range(2)]
    nt_ = 0
    for tb in range(NT):
        ob = tb % 2
        nkeys = 128 * (tb + 1)
        ntile = (nkeys + 511) // 512
        for h in range(4):
            obank = 4 + (tb * 4 + h) % 2
            MEMSET(P, 'gpsimd', cp[:, nkeys + 1:nkeys + 2], 1.0, ['cp'])
            first = True
            for c in range(ntile - 1, -1, -1):
                s0 = 512 * c
                w = min(512, nkeys - s0)
                b = nt_ % 2
                bank = nt_ % 2
                nt_ += 1
                MM(P, C.psf[bank][:, 0:w], qT[0:64, h, tb * 128:(tb + 1) * 128], kT[0:64, h, s0:s0 + w], True, True, ['qT', 'kT'], [('ps', bank)])
                ACT(P, bt[b][:, 0:w], C.psf[bank][:, 0:w], AF.Sigmoid, [('ps', bank)], [('bt', b)], scale=SC)
                ACT(P, om[b][:, 1:w + 1], C.psf[bank][:, 0:w], AF.Sigmoid, [('ps', bank)], [('om', b)], scale=-SC)
                if c == ntile - 1:
                    TT(P, 'vector', bt[b][:, w - 128:w], bt[b][:, w - 128:w], C.mstrict[:], ALU.mult, [('bt', b), 'mstrict'], [('bt', b)])
                    TT(P, 'vector', om[b][:, w - 127:w + 1], om[b][:, w - 127:w + 1], nms, ALU.max, [('om', b), 'nms'], [('om', b)])
                P.op('vector', (lambda o, d0, d1, ini: lambda e: e.tensor_tensor_scan(out=o, data0=d0, data1=d1, initial=ini, op0=ALU.mult, op1=ALU.add))(
                    cp[:, s0 + w:s0:-1], om[b][:, w:0:-1], zeros[:, 0:w], cp[:, s0 + w + 1:s0 + w + 2]),
                    [('om', b), 'zeros', 'cp'], ['cp'])
                TT(P, 'gpsimd', at[b][:, 0:w], bt[b][:, 0:w], cp[:, s0 + 2:s0 + w + 2], ALU.mult, [('bt', b), 'cp'], [('at', b)])
                tbank = 2 + b
                nblk = w // 128
                for j in range(nblk):
                    TR(P, C.psb[tbank][:, j * 128:(j + 1) * 128], at[b][:, j * 128:(j + 1) * 128], C.identb[:], [('at', b)], [('ps', tbank)])
                CP(P, 'scalar', aT[b][:, 0:nblk, :], C.psb[tbank][:, 0:w].rearrange("p (a t) -> p a t", a=nblk), [('ps', tbank)], [('aT', b)])
                for j in range(nblk):
                    sblk = s0 // 128 + j
                    last = (c == 0 and j == nblk - 1)
                    MM(P, C.psf[obank][:, 0:64], aT[b][:, j, :], v[:, sblk, h * 64:(h + 1) * 64], first, last, [('aT', b), 'v'], [('ps', obank)])
                    first = False
            CP(P, 'vector', osb[ob][:, h * 64:(h + 1) * 64], C.psf[obank][:, 0:64], [('ps', obank)], [('osb', ob, h)])
        P.dma('sync', C.osb[tb * 128:(tb + 1) * 128, :], osb[ob], r=[('osb', ob, h) for h in range(4)], w=['osbd'])
    P.barrier()


def phase_hg(P, C, l):
    mb = P.alloc(128, F32)
    md = P.alloc(128, F32)
    mk = P.alloc(128, F32)
    ci = P.alloc(2, F32)
    for nm_, t_ in (('hg_mb', mb), ('hg_md', md), ('hg_mk', mk), ('hg_ci', ci)):
        P.dma('sync', t_, C.d[nm_], w=[nm_])
    lbt = P.alloc(2 * 384, F32).rearrange("p (a c) -> p a c", a=2)
    P.dma('sync', lbt, C.d['hg_lb_b'], w=['lbt'])
    oml = P.alloc(384, F32)
    if l == 0:
        MEMSET(P, 'vector', oml, 1.0, ['oml'])
    else:
        TT(P, 'vector', oml, lbt[:, 0, :], lbt[:, 1, :], ALU.subtract, ['lbt'], ['oml'])
        ACT(P, oml, oml, AF.Sigmoid, ['oml'], ['oml'])
    nw = P.alloc(2 * 384, F32).rearrange("p (a c) -> p a c", a=2)
    P.dma('sync', nw, C.d['hg_nw_b'], w=['nw'])
    St = P.alloc(192, F32)
    MEMSET(P, 'vector', St, 0.0, ['St'])
    Sb = [P.alloc(192, BF16) for _ in range(2)]
    CP(P, 'vector', Sb[0], St, ['St'], [('Sb', 0)])
    inp = [P.alloc(3 * 384, F32).rearrange("p (a c) -> p a c", a=3) for _ in range(2)]
    key = P.alloc(384, F32)
    lf = P.alloc(384, F32)
    ex = P.alloc(4 * 384, F32).rearrange("p (a c) -> p a c", a=4)
    qk = P.alloc(4 * 384, BF16).rearrange("p (a c) -> p a c", a=4)
    vb = P.alloc(384, BF16)
    tT = P.alloc(3 * 3 * 128, BF16).rearrange("p (a j t) -> p a j t", a=3, j=3)
    am = P.alloc(6 * 128, BF16).rearrange("p (h t) -> p h t", h=6)
    ebl = P.alloc(6, F32)
    tmpS = P.alloc(192, F32)
    ot = [P.alloc(384, F32) for _ in range(2)]
    sq = P.alloc(384, F32)
    ms = P.alloc(8, F32)
    S3 = lambda a: a.rearrange("p (j e) -> p j e", j=3)
    eb3 = ebl.rearrange("p (j c) -> p j c", c=2)
    for tt in range(NT):
        b = tt % 2
        rows = slice(tt * 128, (tt + 1) * 128)
        P.dma('sync', inp[b][:, 0, :], C.htok[rows, T_HGQ:T_HGQ + 384], w=[('inp', b)])
        P.dma('sync', inp[b][:, 1, :], C.htok[rows, T_HGF:T_HGF + 384], w=[('inp', b)])
        P.dma('sync', inp[b][:, 2, :], C.htok[rows, T_HGI:T_HGI + 384], w=[('inp', b)])
        ik = [('inp', b)]
        ACT(P, key, inp[b][:, 1, :], AF.Sigmoid, ik, ['key'], scale=-1.0)
        TT(P, 'vector', key, key, oml, ALU.mult, ['key', 'oml'], ['key'])
        ACT(P, lf, key, AF.Ln, ['key'], ['lf'], scale=-1.0, bias=C.eps6[:, 2:3])
        CP(P, 'gpsimd', vb, inp[b][:, 2, :], ik, ['vb'])
        MM(P, C.psf[0][:, 0:384], md, lf, True, True, ['hg_md', 'lf'], [('ps', 0)])
        MM(P, C.psf[1][:, 0:384], mb, lf, True, True, ['hg_mb', 'lf'], [('ps', 1)])
        MM(P, C.psf[2][:, 0:384], mk, lf, True, True, ['hg_mk', 'lf'], [('ps', 2)])
        for j in range(3):
            MM(P, C.psf[3][:, 2 * j:2 * j + 2], lf[:, j * 128:(j + 1) * 128], ci, True, True, ['lf', 'hg_ci'], [('ps', 3)])
        ACT(P, ex[:, 0, :], C.psf[0][:, 0:384], AF.Exp, [('ps', 0)], ['ex0'])
        ACT(P, ex[:, 1, :], C.psf[0][:, 0:384], AF.Exp, [('ps', 0)], ['ex1'], scale=-1.0)
        ACT(P, ex[:, 2, :], C.psf[1][:, 0:384], AF.Exp, [('ps', 1)], ['ex2'])
        ACT(P, ex[:, 3, :], C.psf[2][:, 0:384], AF.Exp, [('ps', 2)], ['ex3'])
        ACT(P, ebl, C.psf[3][:, 0:6], AF.Exp, [('ps', 3)], ['ebl'])
        TT(P, 'vector', qk[:, 0, :], inp[b][:, 0, :], ex[:, 0, :], ALU.mult, ik + ['ex0'], ['qk0'])
        TT(P, 'gpsimd', qk[:, 1, :], key, ex[:, 1, :], ALU.mult, ['key', 'ex1'], ['qk1'])
        TT(P, 'vector', qk[:, 2, :], inp[b][:, 0, :], ex[:, 2, :], ALU.mult, ik + ['ex2'], ['qk2'])
        TT(P, 'gpsimd', qk[:, 3, :], key, ex[:, 3, :], ALU.mult, ['key', 'ex3'], ['qk3'])
        for a in range(3):
            pbk = C.psb[4 + a % 2]
            for j in range(3):
                TR(P, pbk[:, j * 128:(j + 1) * 128], qk[:, a, j * 128:(j + 1) * 128], C.identb[:], [f'qk{a}'], [('ps', 4 + a % 2)])
            CP(P, 'scalar' if a != 1 else 'vector', tT[:, a, :, :], pbk[:, 0:384].rearrange("p (j t) -> p j t", j=3), [('ps', 4 + a % 2)], [f'tT{a}'])
        for h in range(6):
            hh, j = h % 2, h // 2
            pr = slice(hh * 64, hh * 64 + 64)
            MM(P, C.psf[6 + hh][:, j * 128:(j + 1) * 128], tT[pr, 1, j, :], tT[pr, 0, j, :], True, True, ['tT0', 'tT1'], [('ps', 6 + hh)])
        am4 = am.rearrange("p (j hh) t -> p j hh t", hh=2)
        for hh in range(2):
            STT(P, am4[:, :, hh, :], C.psf[6 + hh][:, 0:384].rearrange("p (j t) -> p j t", j=3), 1e30, mb.unsqueeze(1).to_broadcast([128, 3, 128]),
                ALU.min, ALU.mult, [('ps', 6 + hh), 'hg_mb'], [f'am{hh}'])
        for c in range(2):
            cr = slice(c * 64, (c + 1) * 64)
            for h in range(6):
                hh, j = h % 2, h // 2
                MM(P, C.psf[c][hh * 64:(hh + 1) * 64, j * 64:(j + 1) * 64], qk[cr, 3, h * 64:(h + 1) * 64], vb[cr, h * 64:(h + 1) * 64],
                   True, True, ['qk3', 'vb'], [('ps', c)])
        TT(P, 'vector', S3(tmpS), S3(St), eb3[:, :, 0:1].to_broadcast([128, 3, 64]), ALU.mult, ['St', 'ebl'], ['tmpS'])
        TT(P, 'vector', St, tmpS, C.psf[0][:, 0:192], ALU.add, ['tmpS', ('ps', 0)], ['St'])
        CP(P, 'vector', Sb[1], St, ['St'], [('Sb', 1)])
        for h in range(6):
            hh, j = h % 2, h // 2
            pr = slice(hh * 64, hh * 64 + 64)
            oc = slice(h * 64, (h + 1) * 64)
            ob = 2 + hh
            pc = slice(j * 64, (j + 1) * 64)
            MM(P, C.psf[ob][:, pc], am[:, h, :], vb[:, oc], True, False, ['am0', 'am1', 'vb'], [('ps', ob)])
            MM(P, C.psf[ob][0:64, pc], tT[pr, 2, j, 0:64], Sb[0][pr, j * 64:(j + 1) * 64], False, True, ['tT2', ('Sb', 0)], [('ps', ob)])
            MM(P, C.psf[ob][64:128, pc], tT[pr, 2, j, 64:128], Sb[1][pr, j * 64:(j + 1) * 64], False, True, ['tT2', ('Sb', 1)], [('ps', ob)])
        TT(P, 'vector', S3(tmpS), S3(St), eb3[:, :, 1:2].to_broadcast([128, 3, 64]), ALU.mult, ['St', 'ebl'], ['tmpS'])
        TT(P, 'vector', St, tmpS, C.psf[1][:, 0:192], ALU.add, ['tmpS', ('ps', 1)], ['St'])
        CP(P, 'vector', Sb[0], St, ['St'], [('Sb', 0)])
        ot4 = ot[b].rearrange("p (j hh e) -> p j hh e", hh=2, e=64)
        CP(P, 'scalar', ot4[:, :, 0, :], C.psf[2][:, 0:192].rearrange("p (j e) -> p j e", j=3), [('ps', 2)], [('ot', b)])
        CP(P, 'scalar', ot4[:, :, 1, :], C.psf[3][:, 0:192].rearrange("p (j e) -> p j e", j=3), [('ps', 3)], [('ot', b)])
        TT(P, 'gpsimd', sq, ot[b], ot[b], ALU.mult, [('ot', b)], ['sq'])
        P.op('vector', (lambda o, i: lambda e: e.tensor_reduce(out=o, in_=i, axis=AX.X, op=ALU.add))(ms[:, 0:6], sq.rearrange("p (h e) -> p h e", h=6)), ['sq'], ['ms'])
        ACT(P, ms[:, 0:6], ms[:, 0:6], AF.Sqrt, ['ms'], ['ms'], scale=1.0 / 64.0, bias=C.eps6[:, 0:1])
        P.op('vector', (lambda o, i: lambda e: e.reciprocal(o, i))(ms[:, 0:6], ms[:, 0:6]), ['ms'], ['ms'])
        TT(P, 'vector', ot[b].rearrange("p (h e) -> p h e", h=6), ot[b].rearrange("p (h e) -> p h e", h=6),
           ms[:, 0:6].unsqueeze(2).to_broadcast([128, 6, 64]), ALU.mult, [('ot', b), 'ms'], [('ot', b)])
        TT(P, 'vector', ot[b], ot[b], nw[:, l, :], ALU.mult, [('ot', b), 'nw'], [('ot', b)])
        P.dma('gpsimd', C.ohg[rows, :], ot[b], r=[('ot', b)], w=['ohgd'])
    P.barrier()


def phase_out(P, C, l, xsrc, xdst):
    W = P.alloc(8 * DM, BF16).rearrange("p (k c) -> p k c", k=8)
    wsrc = C.d['w_out'][l].rearrange("(k p) c -> p k c", p=128)
    for k in range(8):
        P.dma('gpsimd', W[:, k, :], wsrc[:, k, :], w=[('W', k)])
    wkeys = [('W', k) for k in range(8)]
    lng = P.alloc(DM, F32)
    lnb = P.alloc(DM, F32)
    P.dma('sync', lng, C.d['lng_b'][:, l, :], w=['lng'])
    P.dma('sync', lnb, C.d['lnb_b'][:, l, :], w=['lnb'])
    o = [P.alloc(DM, F32) for _ in range(2)]
    z = [P.alloc(DM, F32) for _ in range(2)]
    xt = [P.alloc(DM, F32) for _ in range(2)]
    mx = P.alloc(DM, BF16)
    mT = P.alloc(8 * 128, BF16).rearrange("p (k t) -> p k t", k=8)
    pre = [P.alloc(DM, F32) for _ in range(2)]
    st = P.alloc(16, F32)
    for tt in range(NT):
        b = tt % 2
        rows = slice(tt * 128, (tt + 1) * 128)
        P.dma('sync', o[b][:, 0:384], C.onsa[rows, :], w=[('o', b)])
        P.dma('sync', o[b][:, 384:640], C.osb[rows, :], w=[('o', b)])
        P.dma('sync', o[b][:, 640:1024], C.ohg[rows, :], w=[('o', b)])
        P.dma('sync', z[b][:, 0:384], C.htok[rows, T_Z:T_Z + 384], w=[('z', b)])
        P.dma('sync', z[b][:, 384:640], C.htok[rows, T_SBZ:T_SBZ + 256], w=[('z', b)])
        P.dma('sync', z[b][:, 640:1024], C.htok[rows, T_HGZ:T_HGZ + 384], w=[('z', b)])
        P.dma('sync', xt[b], xsrc[rows, :], w=[('xt', b)])
        ACT(P, z[b], z[b], AF.Silu, [('z', b)], [('z', b)])
        TT(P, 'vector', mx, o[b], z[b], ALU.mult, [('o', b), ('z', b)], ['mx'])
        pbk = C.psb[4]
        for k in range(8):
            TR(P, pbk[:, k * 128:(k + 1) * 128], mx[:, k * 128:(k + 1) * 128], C.identb[:], ['mx'], [('ps', 4)])
        CP(P, 'scalar', mT[:, 0:4, :], pbk[:, 0:512].rearrange("p (k t) -> p k t", k=4), [('ps', 4)], ['mT'])
        CP(P, 'vector', mT[:, 4:8, :], pbk[:, 512:1024].rearrange("p (k t) -> p k t", k=4), [('ps', 4)], ['mT'])
        for nh in range(2):
            bank = (tt * 2 + nh) % 4
            for k in range(8):
                MM(P, C.psf[bank][:, :], mT[:, k, :], W[:, k, nh * 512:(nh + 1) * 512], k == 0, k == 7, ['mT'] + wkeys, [('ps', bank)])
            STT(P, pre[b][:, nh * 512:(nh + 1) * 512], xt[b][:, nh * 512:(nh + 1) * 512], ALPHA, C.psf[bank][:, :], ALU.mult, ALU.add,
                [('xt', b), ('ps', bank)], [('pre', b, nh)])
            P.op('vector', (lambda o_, i_: lambda e: e.bn_stats(o_, i_))(st[:, nh * 6:(nh + 1) * 6], pre[b][:, nh * 512:(nh + 1) * 512]),
                 [('pre', b, nh)], ['st'])
        P.op('vector', (lambda o_, i_: lambda e: e.bn_aggr(o_, i_))(st[:, 12:14], st[:, 0:12]), ['st'], ['st'])
        ACT(P, st[:, 14:15], st[:, 13:14], AF.Sqrt, ['st'], ['st'], bias=C.eps6[:, 1:2])
        P.op('vector', (lambda o_, i_: lambda e: e.reciprocal(o_, i_))(st[:, 14:15], st[:, 14:15]), ['st'], ['st'])
        pk = [('pre', b, 0), ('pre', b, 1)]
        TS(P, 'vector', pre[b], pre[b], st[:, 12:13], st[:, 14:15], ALU.subtract, ALU.mult, pk + ['st'], pk)
        TT(P, 'gpsimd', pre[b], pre[b], lng, ALU.mult, pk + ['lng'], pk)
        TT(P, 'gpsimd', pre[b], pre[b], lnb, ALU.add, pk + ['lnb'], pk)
        P.dma('gpsimd', xdst[rows, :], pre[b], r=pk, w=['xdst'])
    P.barrier()


def build(debug=False, layers=(0, 1), phases=None, ext=()):
    nc = bass.Bass("TRN2", target_bir_lowering=False)
    P = Prog(nc)
    C = Ctx()
    C.d = {}
    for k, shp in list(IN_SHAPES.items()) + list(CONST_SHAPES.items()):
        C.d[k] = nc.dram_tensor(k, shp, F32, kind="ExternalInput").ap()
    skind = "ExternalOutput" if debug else "Internal"
    sk = lambda n: ("ExternalInput" if n in ext else skind)
    C.htok = nc.dram_tensor("htok", [S, NTOK], F32, kind=sk("htok")).ap()
    C.hfeat = nc.dram_tensor("hfeat", [NFEAT, S], F32, kind=sk("hfeat")).ap()
    C.onsa = nc.dram_tensor("onsa", [S, 384], F32, kind=sk("onsa")).ap()
    C.osb = nc.dram_tensor("osb", [S, 256], F32, kind=sk("osb")).ap()
    C.ohg = nc.dram_tensor("ohg", [S, 384], F32, kind=sk("ohg")).ap()
    C.x1 = nc.dram_tensor("x1", [S, DM], F32, kind=skind).ap()
    C.out = nc.dram_tensor("out", [S, DM], F32, kind="ExternalOutput").ap()
    C.identf = P.sb([128, 128], F32, "identf")
    C.identb = P.sb([128, 128], BF16, "identb")
    C.m0 = P.sb([128, 128], F32, "m0")
    C.m4 = P.sb([128, 128], F32, "m4")
    C.m4b = P.sb([128, 128], BF16, "m4b")
    C.mstrict = P.sb([128, 128], F32, "mstrict")
    C.eps6 = P.sb([128, 4], F32, "eps6")
    C.kcT = P.sb([64, 2, 256], F32, "kcT")
    C.vc = P.sb([128, 2, 2, 64], F32, "vc")
    C.NM = [P.sb([64, S], BF16, f"NM{g}") for g in range(2)]
    C.psf = [P.ps([128, 512], F32, f"psum{i}") for i in range(8)]
    C.psb = [p[:].bitcast(BF16) for p in C.psf]
    P.make_arena(36 * 1024)
    P.dma('sync', C.identf[:], C.d['ident'], w=['identf'])
    P.dma('sync', C.m0[:], C.d['m0'], w=['m0'])
    P.dma('sync', C.m4[:], C.d['m4'], w=['m4'])
    P.dma('sync', C.mstrict[:], C.d['mstrict'], w=['mstrict'])
    CP(P, 'vector', C.identb[:], C.identf[:], ['identf'], ['identb'])
    CP(P, 'vector', C.m4b[:], C.m4[:], ['m4'], ['m4b'])
    MEMSET(P, 'vector', C.eps6[:, 0:1], 1e-6, ['eps6'])
    MEMSET(P, 'vector', C.eps6[:, 1:2], 1e-5, ['eps6'])
    MEMSET(P, 'vector', C.eps6[:, 2:3], 1.0, ['eps6'])
    MEMSET(P, 'vector', C.eps6[:, 3:4], 0.0, ['eps6'])
    MEMSET(P, 'vector', C.kcT[:], 0.0, ['kcT'])
    MEMSET(P, 'vector', C.vc[:], 0.0, ['vc'])
    P.barrier()
    allp = ('proj', 'cmp_mlp', 'cmp_att', 'slcwin', 'sb', 'hg', 'out')
    phases = phases or allp
    for l in layers:
        xsrc = C.d['x'] if l == 0 else C.x1
        xdst = C.out if l == DEPTH - 1 else C.x1
        if 'proj' in phases:
            phase_proj(P, C, l, xsrc)
        if 'cmp_mlp' in phases:
            phase_cmp_mlp(P, C, l)
        if 'cmp_att' in phases:
            phase_cmp_att(P, C, l)
        if 'slcwin' in phases:
            phase_slcwin(P, C, l)
        if 'sb' in phases:
            phase_sb(P, C, l)
        if 'hg' in phases:
            phase_hg(P, C, l)
        if 'out' in phases:
            phase_out(P, C, l, xsrc, xdst)
    P.emit()
    return nc, P


def make_in_maps(inputs, cores=range(8)):
    consts = host_consts()
    consts.update(host_gathers(np.asarray(inputs['rel_bias'], np.float32)))
    f = lambda a: np.ascontiguousarray(np.asarray(a, np.float32))
    shared = {
        'w_in': f(inputs['w_in']), 'cmp_posT': f(np.asarray(inputs['cmp_pos']).transpose(0, 2, 1)),
        'w_ck1': f(inputs['w_ck1']), 'w_ck2': f(inputs['w_ck2']), 'w_cv1': f(inputs['w_cv1']), 'w_cv2': f(inputs['w_cv2']),
        'hg_lb_b': f(np.broadcast_to(np.asarray(inputs['hg_lb'])[None], (128, DEPTH, 384))),
        'hg_nw_b': f(np.broadcast_to(np.asarray(inputs['hg_norm_w'])[None], (128, DEPTH, 384))),
        'w_out': f(inputs['w_out']),
        'lng_b': f(np.broadcast_to(np.asarray(inputs['ln_g'])[None], (128, DEPTH, DM))),
        'lnb_b': f(np.broadcast_to(np.asarray(inputs['ln_b'])[None], (128, DEPTH, DM))),
    }
    for k, v in consts.items():
        shared[k] = f(v)
    x = np.asarray(inputs['x'], np.float32)
    maps = []
    for c in cores:
        m = dict(shared)
        m['x'] = np.ascontiguousarray(x[c])
        maps.append(m)
    return maps


_CACHE = {}


def kernel(**inputs):
    if 'nc' not in _CACHE:
        _CACHE['nc'] = build()[0]
    nc = _CACHE['nc']
    maps = make_in_maps(inputs)
    res = run_bass_kernel_spmd(nc, maps, core_ids=list(range(8)))
    out = np.stack([np.asarray(r['out'], np.float32) for r in res.results], axis=0)
    return out
```

```python
import math
import numpy as np
from contextlib import ExitStack
import concourse.bass as bass
import concourse.mybir as mybir
from concourse.bass_utils import run_bass_kernel_spmd

F32 = mybir.dt.float32
BF16 = mybir.dt.bfloat16
AF = mybir.ActivationFunctionType
ALU = mybir.AluOpType
AX = mybir.AxisListType

S = 4096
DM = 1024
DEPTH = 2
D_IN = 4114
NT = S // 128
ALPHA = (2 * DEPTH) ** 0.25
SC = 0.125
BIG = 16384.0

NSLOT = 8
DMA_QUEUES = ('sync', 'gpsimd')
COMPUTE = ('tensor', 'vector', 'scalar', 'gpsimd')
ALLENG = ('sync', 'tensor', 'vector', 'scalar', 'gpsimd')


class Prog:
    def __init__(self, nc, same_engine_sync=True):
        self.nc = nc
        self.ops = []
        self.stack = ExitStack()
        self.same_engine_sync = same_engine_sync
        self._n = 0
        self.arena = None
        self.apos = 0

    def sb(self, shape, dtype, name=None):
        self._n += 1
        return self.stack.enter_context(self.nc.sbuf_tensor("S_" + (name or f"sb{self._n}"), list(shape), dtype))

    def ps(self, shape, dtype, name=None):
        self._n += 1
        return self.stack.enter_context(self.nc.psum_tensor("P_" + (name or f"ps{self._n}"), list(shape), dtype))

    def make_arena(self, nf32):
        self.arena = self.sb([128, nf32], F32, "arena")
        self.asize = nf32

    def alloc(self, n, dtype):
        w = n if dtype == F32 else (n + 1) // 2
        w = (w + 7) // 8 * 8
        assert self.apos + w <= self.asize, (self.apos, w, self.asize)
        a = self.arena[:, self.apos:self.apos + w]
        self.apos += w
        if dtype == F32:
            return a[:, 0:n]
        return a.bitcast(BF16)[:, 0:n]

    def op(self, eng, fn, r=(), w=()):
        self.ops.append((eng, fn, tuple(r), tuple(w), False))

    def dma(self, queue, out, in_, r=(), w=(), **kw):
        if out.dtype != in_.dtype:
            assert queue == 'gpsimd'
            kw.setdefault('max_dma_last_dim', 4096)
        self.ops.append((queue, (out, in_, kw), tuple(r), tuple(w), True))

    def barrier(self):
        self.ops.append(('BARRIER', None, (), (), False))
        self.apos = 0

    def emit(self):
        nc = self.nc
        st = self.stack
        sem_c = {e: st.enter_context(nc.semaphore(f"s_{e}")) for e in COMPUTE}
        sem_d = {q: [st.enter_context(nc.semaphore(f"d_{q}{i}")) for i in range(NSLOT)] for q in DMA_QUEUES}
        cnt_c = {e: 0 for e in COMPUTE}
        cnt_d = {q: 0 for q in DMA_QUEUES}
        last_w = {}
        readers = {}
        token = []
        streams = {e: [] for e in ALLENG}
        waited = {e: {} for e in ALLENG}
        semobj = {}
        for e in COMPUTE:
            semobj[('c', e)] = sem_c[e]
        for q in DMA_QUEUES:
            for i in range(NSLOT):
                semobj[('d', q, i)] = sem_d[q][i]

        def need(eng, tok):
            sk, v = tok
            if v <= 0 or waited[eng].get(sk, 0) >= v:
                return
            waited[eng][sk] = v
            streams[eng].append(('wait', sk, v))

        def all_tokens():
            toks = []
            for q in DMA_QUEUES:
                n = cnt_d[q]
                for slot in range(NSLOT):
                    if n > slot:
                        m_last = ((n - 1 - slot) // NSLOT) * NSLOT + slot
                        toks.append((('d', q, slot), 16 * (m_last // NSLOT + 1)))
            for e in COMPUTE:
                if cnt_c[e]:
                    toks.append((('c', e), cnt_c[e]))
            return toks

        for i, (eng, fn, rs, ws, is_dma) in enumerate(self.ops):
            if eng == 'BARRIER':
                toks = all_tokens()
                for e in ALLENG:
                    for tk in toks:
                        need(e, tk)
                last_w.clear()
                readers.clear()
                token.append(None)
                continue
            deps = set()
            raw = set()
            for k in rs:
                if k in last_w:
                    deps.add(last_w[k])
                    raw.add(last_w[k])
            for k in ws:
                if k in last_w:
                    deps.add(last_w[k])
                for j in readers.get(k, ()):
                    deps.add(j)
            for j in sorted(deps):
                je, _, _, _, jdma = self.ops[j]
                if (not jdma) and je == eng and not is_dma:
                    if eng == 'tensor' or not self.same_engine_sync or (eng in ('vector', 'scalar') and j not in raw):
                        continue
                need(eng, token[j])
            if is_dma:
                m = cnt_d[eng]
                cnt_d[eng] += 1
                slot = m % NSLOT
                sk = ('d', eng, slot)
                if m >= NSLOT:
                    need(eng, (sk, 16 * (m // NSLOT)))
                tok = (sk, 16 * (m // NSLOT + 1))
                streams[eng].append(('dma', fn, sk))
            else:
                cnt_c[eng] += 1
                sk = ('c', eng)
                tok = (sk, cnt_c[eng])
                streams[eng].append(('op', fn, sk))
            token.append(tok)
            for k in ws:
                last_w[k] = i
                readers[k] = []
            for k in rs:
                if k not in ws:
                    readers.setdefault(k, []).append(i)
        for tk in all_tokens():
            need('sync', tk)

        def run_stream(name, engobj):
            for item in streams[name]:
                if item[0] == 'wait':
                    engobj.wait_ge(semobj[item[1]], item[2])
                elif item[0] == 'dma':
                    out, in_, kw = item[1]
                    engobj.dma_start(out=out, in_=in_, **kw).then_inc(semobj[item[2]], 16)
                else:
                    item[1](engobj).then_inc(semobj[item[2]], 1)

        with nc.Block() as block:
            @block.sync
            def _(e):
                run_stream('sync', e)

            @block.tensor
            def _(e):
                run_stream('tensor', e)

            @block.vector
            def _(e):
                run_stream('vector', e)

            @block.scalar
            def _(e):
                run_stream('scalar', e)

            @block.gpsimd
            def _(e):
                run_stream('gpsimd', e)
        self.stack.close()
        self.counts = (dict(cnt_c), dict(cnt_d))


def MM(P, out, lhsT, rhs, start, stop, r, w):
    P.op('tensor', lambda e: e.matmul(out, lhsT, rhs, start=start, stop=stop), r, w)


def TR(P, out, in_, ident, r, w):
    P.op('tensor', lambda e: e.transpose(out, in_, ident), r, w)


def ACT(P, out, in_, func, r, w, bias=None, scale=None, accum=None):
    kw = {}
    if bias is not None:
        kw['bias'] = bias
    if scale is not None:
        kw['scale'] = scale
    if accum is not None:
        kw['accum_out'] = accum
    P.op('scalar', lambda e: e.activation(out, in_, func, **kw), r, w)


def TT(P, eng, out, in0, in1, op, r, w):
    P.op(eng, lambda e: e.tensor_tensor(out=out, in0=in0, in1=in1, op=op), r, w)


def TS(P, eng, out, in0, s1, s2, op0, op1, r, w):
    if op1 is None:
        P.op(eng, lambda e: e.tensor_scalar(out, in0, s1, None, op0), r, w)
    else:
        P.op(eng, lambda e: e.tensor_scalar(out, in0, s1, s2, op0, op1), r, w)


def STT(P, out, in0, scalar, in1, op0, op1, r, w):
    P.op('vector', lambda e: e.scalar_tensor_tensor(out=out, in0=in0, scalar=scalar, in1=in1, op0=op0, op1=op1), r, w)


def CP(P, eng, out, in_, r, w):
    if eng == 'scalar':
        P.op('scalar', lambda e: e.copy(out, in_), r, w)
    else:
        P.op(eng, lambda e: e.tensor_copy(out=out, in_=in_), r, w)


def MEMSET(P, eng, ap, val, w):
    P.op(eng, lambda e: e.memset(ap, val), (), w)


def t5_bucket_np(rel):
    n = np.maximum(rel, 0)
    nf = np.maximum(n, 1).astype(np.float32)
    large = 16 + (np.log(nf / np.float32(16)) / np.float32(math.log(8.0)) * np.float32(16)).astype(np.int32)
    large = np.clip(large, 0, 31)
    return np.where(n < 16, n, large).astype(np.int64)


def host_consts():
    c = {}
    c['ident'] = np.eye(128, dtype=np.float32)
    sl = np.arange(128)[:, None]
    tl = np.arange(128)[None, :]
    c['m0'] = (tl >= sl).astype(np.float32)
    c['m4'] = (sl > tl).astype(np.float32)
    c['mstrict'] = (tl < sl).astype(np.float32)
    ch_s = sl // 64
    ch_t = tl // 64
    same = (ch_s == ch_t)
    mid_t = ch_t * 64 + 31
    c['hg_mb'] = (same & (sl <= tl)).astype(np.float32)
    c['hg_md'] = (same * ((sl <= tl).astype(np.float32) - (sl <= mid_t).astype(np.float32))).astype(np.float32)
    c['hg_mk'] = (same & (sl > tl)).astype(np.float32)
    ci = np.zeros((128, 2), np.float32)
    ci[:64, 0] = 1
    ci[64:, 1] = 1
    c['hg_ci'] = ci
    n = np.arange(256)
    j = np.arange(64)
    cs = n[:, None] * 16
    ov = ((cs < j[None, :] * 64 + 64) & (cs + 32 > j[None, :] * 64)).astype(np.float32)
    ov[255] = 0
    c['ov'] = np.ascontiguousarray(ov.reshape(2, 128, 64).transpose(1, 0, 2))
    t = np.arange(S)
    cur = t // 64
    forced = (j[None, :] == 0) | (j[None, :] == cur[:, None]) | (j[None, :] == cur[:, None] - 1)
    valid = j[None, :] * 64 <= t[:, None]
    fc = np.where(valid, 1000.0 * forced, -1e30).astype(np.float32)
    c['fc'] = np.ascontiguousarray(fc.reshape(32, 128, 64).transpose(1, 0, 2))
    ew = np.zeros((64, S), np.float32)
    ew[np.arange(S) // 64, np.arange(S)] = BIG
    c['ew'] = ew
    rv = np.ones((128, 1), np.float32)
    rv[:31] = 0
    c['rv0'] = rv
    return c


def host_gathers(rel_bias):
    g = {}
    tl = np.arange(128)[:, None]
    m = np.arange(511)[None, :]
    rel = tl - 16 * (m - 255) - 31
    wc = rel_bias[t5_bucket_np(rel)]
    wc = np.where((rel >= 0)[:, :, None], wc, np.float32(-30000.0)).astype(np.float32)
    g['wc'] = np.ascontiguousarray(wc.transpose(0, 2, 1))
    sl = np.arange(128)[:, None]
    t2 = np.arange(128)[None, :]
    tz0 = rel_bias[t5_bucket_np(t2 - sl)]
    tz1 = rel_bias[t5_bucket_np(128 + t2 - sl)]
    g['tz'] = np.ascontiguousarray(np.stack([tz0, tz1], 0).transpose(1, 3, 0, 2)).astype(np.float32)
    g['b31'] = np.ascontiguousarray(np.broadcast_to(rel_bias[31][None, :], (128, 6))).astype(np.float32)
    return g


CONST_SHAPES = {
    'ident': [128, 128], 'm0': [128, 128], 'm4': [128, 128], 'mstrict': [128, 128],
    'hg_mb': [128, 128], 'hg_md': [128, 128], 'hg_mk': [128, 128], 'hg_ci': [128, 2],
    'ov': [128, 2, 64], 'fc': [128, 32, 64], 'ew': [64, S], 'rv0': [128, 1],
    'wc': [128, 6, 511], 'tz': [128, 6, 2, 128], 'b31': [128, 6],
}
IN_SHAPES = {
    'x': [S, DM], 'w_in': [DEPTH, DM, D_IN], 'cmp_posT': [DEPTH, 64, 32],
    'w_ck1': [DEPTH, 2048, 128], 'w_ck2': [DEPTH, 128, 64], 'w_cv1': [DEPTH, 2048, 128], 'w_cv2': [DEPTH, 128, 64],
    'hg_lb_b': [128, DEPTH, 384], 'hg_nw_b': [128, DEPTH, 384], 'w_out': [DEPTH, DM, DM],
    'lng_b': [128, DEPTH, DM], 'lnb_b': [128, DEPTH, DM],
}
FEAT_CHUNK_SRC = [0, 128, 256, 384, 512, 640, 896, 1554, 1682, 1810, 1938]
F_Q, F_KC, F_VC, F_KS, F_KW, F_SBQ, F_SBK = 0, 384, 512, 640, 768, 896, 1152
NFEAT = 1408
TOK_PIECES = [(768, 128, 0), (1024, 128, 128), (1152, 402, 256), (2066, 512, 658), (2578, 512, 1170),
              (3090, 512, 1682), (3602, 512, 2194)]
T_VS, T_VW, T_G, T_Z, T_SBV, T_SBZ, T_HGQ, T_HGF, T_HGI, T_HGZ = 0, 128, 256, 274, 658, 914, 1170, 1554, 1938, 2322
NTOK = 2706


class Ctx:
    pass


def phase_proj(P, C, l, xsrc):
    W = P.alloc(8 * D_IN, BF16).rearrange("p (k c) -> p k c", k=8)
    xT = P.alloc(8 * S, BF16).rearrange("p (k t) -> p k t", k=8)
    wsrc = C.d['w_in'][l].rearrange("(k p) c -> p k c", p=128)
    for k in range(8):
        for c0 in range(0, D_IN, 2048):
            c1 = min(D_IN, c0 + 2048)
            P.dma('gpsimd', W[:, k, c0:c1], wsrc[:, k, c0:c1], w=[('W', k)])
    xb = [P.alloc(DM, BF16) for _ in range(2)]
    fst = [P.alloc(512, F32) for _ in range(4)]
    for tt in range(NT):
        b = tt % 2
        P.dma('gpsimd', xb[b], xsrc[tt * 128:(tt + 1) * 128, :], w=[('xb', b)])
        pb = C.psb[6 + b]
        for k in range(8):
            TR(P, pb[:, k * 128:(k + 1) * 128], xb[b][:, k * 128:(k + 1) * 128], C.identb[:], [('xb', b)], [('ps', 6 + b)])
        CP(P, 'vector', xT[:, 0:4, tt * 128:(tt + 1) * 128], pb[:, 0:512].rearrange("p (k t) -> p k t", k=4), [('ps', 6 + b)], [('xT', tt)])
        CP(P, 'scalar', xT[:, 4:8, tt * 128:(tt + 1) * 128], pb[:, 512:1024].rearrange("p (k t) -> p k t", k=4), [('ps', 6 + b)], [('xT', tt)])
    wkeys = [('W', k) for k in range(8)]
    nps = 0
    ne = 0
    for tt in range(NT):
        for pi, (c0, n, d0) in enumerate(TOK_PIECES):
            bank = nps % 6
            nps += 1
            for k in range(8):
                MM(P, C.psf[bank][:, 0:n], xT[:, k, tt * 128:(tt + 1) * 128], W[:, k, c0:c0 + n], k == 0, k == 7,
                   [('xT', tt)] + wkeys, [('ps', bank)])
            b = ne % 4
            ne += 1
            CP(P, 'vector' if ne % 2 == 0 else 'scalar', fst[b][:, 0:n], C.psf[bank][:, 0:n], [('ps', bank)], [('fst', b)])
            P.dma('sync', C.htok[tt * 128:(tt + 1) * 128, d0:d0 + n], fst[b][:, 0:n], r=[('fst', b)], w=['htok'])
    for t0 in range(0, S, 512):
        xk = [('xT', tt) for tt in range(t0 // 128, t0 // 128 + 4)]
        for ci, src in enumerate(FEAT_CHUNK_SRC):
            bank = nps % 6
            nps += 1
            for k in range(8):
                MM(P, C.psf[bank][:, :], W[:, k, src:src + 128], xT[:, k, t0:t0 + 512], k == 0, k == 7, xk + wkeys, [('ps', bank)])
            b = ne % 4
            ne += 1
            CP(P, 'vector' if ne % 2 == 0 else 'scalar', fst[b], C.psf[bank][:, :], [('ps', bank)], [('fst', b)])
            P.dma('sync', C.hfeat[ci * 128:(ci + 1) * 128, t0:t0 + 512], fst[b], r=[('fst', b)], w=['hfeat'])
    P.barrier()


def gelu_tanh(P, out, u, tmp, key):
    TT(P, 'vector', tmp, u, u, ALU.mult, [key + 'u'], [key + 't'])
    TS(P, 'vector', tmp, tmp, 0.044715, 1.0, ALU.mult, ALU.add, [key + 't'], [key + 't'])
    TT(P, 'vector', tmp, tmp, u, ALU.mult, [key + 't', key + 'u'], [key + 't'])
    ACT(P, tmp, tmp, AF.Tanh, [key + 't'], [key + 't'], scale=0.7978845608028654)
    TS(P, 'vector', tmp, tmp, 1.0, 0.5, ALU.add, ALU.mult, [key + 't'], [key + 't'])
    TT(P, 'vector', out, tmp, u, ALU.mult, [key + 't', key + 'u'], [key + 'o'])


def phase_cmp_mlp(P, C, l):
    src = P.alloc(2 * S, F32).rearrange("p (a t) -> p a t", a=2)
    P.dma('sync', src[:, 0, :], C.hfeat[F_KC:F_KC + 128, :], w=['src0'])
    P.dma('sync', src[:, 1, :], C.hfeat[F_VC:F_VC + 128, :], w=['src1'])
    w1 = P.alloc(2 * 32 * 128, F32).rearrange("p (a q h) -> p a q h", a=2, q=32)
    w2 = P.alloc(2 * 64, F32).rearrange("p (a h) -> p a h", a=2)
    cpT = P.alloc(32, F32)
    for a, (n1, n2) in enumerate((('w_ck1', 'w_ck2'), ('w_cv1', 'w_cv2'))):
        for half in range(2):
            for q0 in range(0, 32, 8):
                P.dma('sync', w1[half * 64:(half + 1) * 64, a, q0:q0 + 8], C.d[n1][l][q0 * 64:(q0 + 8) * 64, :].rearrange("(q d) h -> d q h", d=64), w=['w1'])
        P.dma('sync', w2[:, a], C.d[n2][l], w=['w2'])
    for half in range(2):
        P.dma('sync', cpT[half * 64:(half + 1) * 64, :], C.d['cmp_posT'][l], w=['cpT'])
    u = P.alloc(256, F32)
    tmp = P.alloc(256, F32)
    g1 = P.alloc(256, F32)
    c1 = P.alloc(1, F32)
    ev = P.alloc(256, F32)
    for a in range(2):
        for g in range(2):
            pr = slice(g * 64, (g + 1) * 64)
            key = f"cm{a}{g}"
            for q in range(32):
                MM(P, C.psf[4 * g + 0][:, 0:255], w1[pr, a, q, :], src[pr, a, q:q + 16 * 254 + 1:16], q == 0, q == 31,
                   ['w1', f'src{a}'], [('ps', 4 * g + 0)])
            for q in range(32):
                MM(P, C.psf[4 * g + 1][:, 0:1], w1[pr, a, q, :], cpT[pr, q:q + 1], q == 0, q == 31, ['w1', 'cpT'], [('ps', 4 * g + 1)])
            CP(P, 'vector', c1, C.psf[4 * g + 1][:, 0:1], [('ps', 4 * g + 1)], [key + 'c1'])
            ACT(P, u[:, 0:255], C.psf[4 * g + 0][:, 0:255], AF.Identity, [('ps', 4 * g + 0), key + 'c1'], [key + 'u'], bias=c1)
            gelu_tanh(P, g1[:, 0:255], u[:, 0:255], tmp[:, 0:255], key)
            if a == 0:
                MM(P, C.psf[4 * g + 2][0:64, 0:255], w2[:, a, :], g1[:, 0:255], True, True, ['w2', key + 'o'], [('ps', 4 * g + 2)])
                CP(P, 'vector', C.kcT[0:64, g, 0:255], C.psf[4 * g + 2][0:64, 0:255], [('ps', 4 * g + 2)], ['kcT'])
            else:
                for nt in range(2):
                    n = 128 if nt == 0 else 127
                    MM(P, C.psf[4 * g + 3][0:n, nt * 64:(nt + 1) * 64], g1[:, nt * 128:nt * 128 + n], w2[:, a, :], True, True,
                       ['w2', key + 'o'], [('ps', 4 * g + 3)])
                    CP(P, 'vector', C.vc[0:n, g, nt, :], C.psf[4 * g + 3][0:n, nt * 64:(nt + 1) * 64], [('ps', 4 * g + 3)], ['vc'])
    P.barrier()


def phase_cmp_att(P, C, l):
    wc = P.alloc(6 * 511, F32).rearrange("p (h m) -> p h m", h=6)
    P.dma('sync', wc, C.d['wc'], w=['wc'])
    fc = P.alloc(32 * 64, F32).rearrange("p (a j) -> p a j", a=32)
    P.dma('sync', fc, C.d['fc'], w=['fc'])
    ov = P.alloc(128, F32).rearrange("p (a j) -> p a j", a=2)
    P.dma('sync', ov, C.d['ov'], w=['ov'])
    rv0 = P.alloc(1, F32)
    P.dma('sync', rv0, C.d['rv0'], w=['rv0'])
    qT = [P.alloc(6 * 128, F32).rearrange("p (h t) -> p h t", h=6) for _ in range(2)]
    gl = [P.alloc(18, F32) for _ in range(2)]
    on = [P.alloc(384, F32) for _ in range(2)]
    sbuf = [P.alloc(256, F32) for _ in range(2)]
    ebuf = [P.alloc(256, F32) for _ in range(2)]
    pbuf = [P.alloc(256, F32) for _ in range(2)]
    pT = [P.alloc(256, F32).rearrange("p (a t) -> p a t", a=2) for _ in range(2)]
    pg = P.alloc(256, F32)
    pgT = P.alloc(256, F32).rearrange("p (a t) -> p a t", a=2)
    sm = [P.alloc(8, F32) for _ in range(2)]
    sc = P.alloc(64, F32)
    sc2 = P.alloc(64, F32)
    m8 = P.alloc(16, F32)
    nm = P.alloc(64, F32)
    qsrc = C.hfeat[F_Q:F_Q + 384, :].rearrange("(h d) t -> d h t", d=64)
    hh = 0
    for tb in range(NT):
        b = tb % 2
        P.dma('sync', qT[b][0:64], qsrc[:, :, tb * 128:(tb + 1) * 128], w=[('qT', b)])
        P.dma('sync', gl[b], C.htok[tb * 128:(tb + 1) * 128, T_G:T_G + 18], w=[('gl', b)])
        ACT(P, gl[b], gl[b], AF.Sigmoid, [('gl', b)], [('gl', b)])
        for g in range(2):
            for hp in range(3):
                h = g * 3 + hp
                hb = hh % 2
                hh += 1
                bank = hb
                MM(P, C.psf[bank][:, 0:256], qT[b][0:64, h, :], C.kcT[0:64, g, :], True, True, [('qT', b), 'kcT'], [('ps', bank)])
                STT(P, sbuf[hb], C.psf[bank][:, 0:256], SC, wc[:, h, 255 - 8 * tb:255 - 8 * tb + 256], ALU.mult, ALU.add,
                    [('ps', bank), 'wc'], [('sbuf', hb)])
                P.op('vector', (lambda o, i: lambda e: e.tensor_reduce(out=o, in_=i, axis=AX.X, op=ALU.max, negate=True))(sm[hb][:, 0:1], sbuf[hb]),
                     [('sbuf', hb)], [('sm0', hb)])
                ACT(P, ebuf[hb], sbuf[hb], AF.Exp, [('sbuf', hb), ('sm0', hb)], [('ebuf', hb)], bias=sm[hb][:, 0:1])
                P.op('vector', (lambda o, i: lambda e: e.tensor_reduce(out=o, in_=i, axis=AX.X, op=ALU.add))(sm[hb][:, 1:2], ebuf[hb]),
                     [('ebuf', hb)], [('sm1', hb)])
                P.op('vector', (lambda o, i: lambda e: e.reciprocal(o, i))(sm[hb][:, 2:3], sm[hb][:, 1:2]), [('sm1', hb)], [('sm2', hb)])
                if tb == 0:
                    TT(P, 'vector', sm[hb][:, 2:3], sm[hb][:, 2:3], rv0, ALU.mult, [('sm2', hb), 'rv0'], [('sm2', hb)])
                TS(P, 'vector', pbuf[hb], ebuf[hb], sm[hb][:, 2:3], None, ALU.mult, None, [('ebuf', hb), ('sm2', hb)], [('pbuf', hb)])
                if hp == 0:
                    CP(P, 'gpsimd', pg, pbuf[hb], [('pbuf', hb)], ['pg'])
                else:
                    TT(P, 'gpsimd', pg, pg, pbuf[hb], ALU.add, ['pg', ('pbuf', hb)], ['pg'])
                tbank = 2 + hb
                for a in range(2):
                    TR(P, C.psf[tbank][:, a * 128:(a + 1) * 128], pbuf[hb][:, a * 128:(a + 1) * 128], C.identf[:], [('pbuf', hb)], [('ps', tbank)])
                CP(P, 'scalar', pT[hb], C.psf[tbank][:, 0:256].rearrange("p (a t) -> p a t", a=2), [('ps', tbank)], [('pT', hb)])
                obank = 4 + hb
                for a in range(2):
                    MM(P, C.psf[obank][:, 0:64], pT[hb][:, a, :], C.vc[:, g, a, :], a == 0, a == 1, [('pT', hb), 'vc'], [('ps', obank)])
                TS(P, 'vector', on[b][:, h * 64:(h + 1) * 64], C.psf[obank][:, 0:64], gl[b][:, 3 * h:3 * h + 1], None, ALU.mult, None,
                   [('ps', obank), ('gl', b)], [('on', b, h)])
            for a in range(2):
                TR(P, C.psf[6][:, a * 128:(a + 1) * 128], pg[:, a * 128:(a + 1) * 128], C.identf[:], ['pg'], [('ps', 6)])
            CP(P, 'scalar', pgT, C.psf[6][:, 0:256].rearrange("p (a t) -> p a t", a=2), [('ps', 6)], ['pgT'])
            for a in range(2):
                MM(P, C.psf[7][:, 0:64], pgT[:, a, :], ov[:, a, :], a == 0, a == 1, ['pgT', 'ov'], [('ps', 7)])
            TT(P, 'vector', sc, C.psf[7][:, 0:64], fc[:, tb, :], ALU.add, [('ps', 7), 'fc'], ['sc'])
            P.op('vector', (lambda o, i: lambda e: e.max(out=o, in_=i))(m8[:, 0:8], sc), ['sc'], ['m8a'])
            P.op('vector', (lambda o, r_, v: lambda e: e.match_replace(out=o, in_to_replace=r_, in_values=v, imm_value=-3.0e38))(sc2, m8[:, 0:8], sc),
                 ['sc', 'm8a'], ['sc2'])
            P.op('vector', (lambda o, i: lambda e: e.max(out=o, in_=i))(m8[:, 8:16], sc2), ['sc2'], ['m8b'])
            TS(P, 'vector', nm, sc, m8[:, 15:16], 1.0, ALU.is_ge, ALU.subtract, ['sc', 'm8b'], ['nm'])
            TR(P, C.psf[7][0:64, 128:256], nm, C.identf[:], ['nm'], [('ps', 7)])
            CP(P, 'vector', C.NM[g][0:64, tb * 128:(tb + 1) * 128], C.psf[7][0:64, 128:256], [('ps', 7)], [('NM', g)])
        P.dma('gpsimd', C.onsa[tb * 128:(tb + 1) * 128, :], on[b], r=[('on', b, h) for h in range(6)], w=['onsa'])
    P.barrier()


def phase_slcwin(P, C, l):
    tz = P.alloc(6 * 2 * 128, F32).rearrange("p (h a t) -> p h a t", h=6, a=2)
    P.dma('sync', tz, C.d['tz'], w=['tz'])
    b31 = P.alloc(6, F32)
    P.dma('sync', b31, C.d['b31'], w=['b31'])
    TS(P, 'vector', b31, b31, -1.0, None, ALU.mult, None, ['b31'], ['b31'])
    cc = P.alloc(6 * 2 * 128, BF16).rearrange("p (h a t) -> p h a t", h=6, a=2)
    for h in range(6):
        for a in range(2):
            ACT(P, tz[:, h, a, :], tz[:, h, a, :], AF.Exp, ['tz', 'b31'], ['tz'], bias=b31[:, h:h + 1])
    for h in range(6):
        TT(P, 'vector', cc[:, h, 0, :], tz[:, h, 0, :], C.m0[:], ALU.mult, ['tz', 'm0'], ['cc'])
        CP(P, 'vector', cc[:, h, 1, :], tz[:, h, 1, :], ['tz'], ['cc'])
    ew = P.alloc(S, BF16)
    P.dma('gpsimd', ew[0:64], C.d['ew'], w=['ew'])
    ksT = P.alloc(S, BF16)
    kwT = P.alloc(S, BF16)
    vs = P.alloc(32 * 65, BF16).rearrange("p (a e) -> p a e", a=32)
    vw = P.alloc(32 * 65, BF16).rearrange("p (a e) -> p a e", a=32)
    qT = [P.alloc(S, BF16) for _ in range(2)]
    gl = P.alloc(32 * 18, F32).rearrange("p (a e) -> p a e", a=32)
    for a_ in range(32):
        P.dma('sync', gl[:, a_, :], C.htok[a_ * 128:(a_ + 1) * 128, T_G:T_G + 18], w=['gl'])
    ACT(P, gl, gl, AF.Sigmoid, ['gl'], ['gl'])
    PTs2 = [P.alloc(32 * 512, BF16).rearrange("p (a t) -> p a t", a=32) for _ in range(2)]
    PTw2 = [P.alloc(8 * 512, BF16).rearrange("p (a t) -> p a t", a=8) for _ in range(2)]
    on = [P.alloc(384, F32) for _ in range(2)]
    onl = [P.alloc(384, F32) for _ in range(2)]
    sm = [P.alloc(4, F32) for _ in range(2)]
    nps = 0
    nob = 0
    for h in range(6):
        g = h // 3
        hb = h % 2
        if h % 3 == 0:
            P.dma('gpsimd', ksT[0:64], C.hfeat[F_KS + g * 64:F_KS + (g + 1) * 64, :], w=['ksT'])
            P.dma('gpsimd', kwT[0:64], C.hfeat[F_KW + g * 64:F_KW + (g + 1) * 64, :], w=['kwT'])
            MEMSET(P, 'vector', vs[:, :, 64:65], 1.0, ['vs'])
            MEMSET(P, 'vector', vw[:, :, 64:65], 1.0, ['vw'])
            for a_ in range(0, 32, 8):
                P.dma('gpsimd', vs[:, a_:a_ + 8, 0:64], C.htok[a_ * 128:(a_ + 8) * 128, T_VS + g * 64:T_VS + (g + 1) * 64].rearrange("(a p) e -> p a e", p=128), r=['vs'], w=['vs'])
                P.dma('gpsimd', vw[:, a_:a_ + 8, 0:64], C.htok[a_ * 128:(a_ + 8) * 128, T_VW + g * 64:T_VW + (g + 1) * 64].rearrange("(a p) e -> p a e", p=128), r=['vw'], w=['vw'])
        P.dma('gpsimd', qT[hb][0:64], C.hfeat[F_Q + h * 64:F_Q + (h + 1) * 64, :], w=[('qT', hb)])
        for qt in range(8):
            t0 = qt * 512
            pb = (h * 8 + qt) % 2
            PTs, PTw = PTs2[pb], PTw2[pb]
            for kt in range(4 * qt + 4):
                bank = nps % 4
                nps += 1
                c0 = max(0, kt - 4 * qt) * 128
                MM(P, C.psf[bank][:, c0:512], ksT[0:64, kt * 128:(kt + 1) * 128], qT[hb][0:64, t0 + c0:t0 + 512], True, False,
                   ['ksT', ('qT', hb)], [('ps', bank)])
                MM(P, C.psf[bank][:, c0:512], ew[0:64, kt * 128:(kt + 1) * 128], C.NM[g][0:64, t0 + c0:t0 + 512], False, True,
                   ['ew', ('NM', g)], [('ps', bank)])
                ACT(P, PTs[:, kt, c0:512], C.psf[bank][:, c0:512], AF.Exp, [('ps', bank)], [('PTs', pb, kt)], scale=SC)
                for tq in range(4):
                    dl = 4 * qt + tq - kt
                    if dl in (0, 1):
                        TT(P, 'gpsimd', PTs[:, kt, tq * 128:(tq + 1) * 128], PTs[:, kt, tq * 128:(tq + 1) * 128], cc[:, h, dl, :], ALU.mult,
                           [('PTs', pb, kt), 'cc'], [('PTs', pb, kt)])
            kts = [kt for kt in range(4 * qt - 4, 4 * qt + 4) if kt >= 0]
            for kt in kts:
                j = kt - (4 * qt - 4)
                bank = nps % 4
                nps += 1
                lo = max(0, kt - 4 * qt)
                hi = min(3, kt + 4 - 4 * qt)
                c0, c1 = lo * 128, (hi + 1) * 128
                MM(P, C.psf[bank][:, c0:c1], kwT[0:64, kt * 128:(kt + 1) * 128], qT[hb][0:64, t0 + c0:t0 + c1], True, True,
                   ['kwT', ('qT', hb)], [('ps', bank)])
                ACT(P, PTw[:, j, c0:c1], C.psf[bank][:, c0:c1], AF.Exp, [('ps', bank)], [('PTw', pb, j)], scale=SC)
                for tq in range(lo, hi + 1):
                    dl = 4 * qt + tq - kt
                    if dl in (0, 1):
                        msk = cc[:, h, dl, :]
                    elif dl == 4:
                        msk = C.m4b[:]
                    else:
                        continue
                    TT(P, 'gpsimd', PTw[:, j, tq * 128:(tq + 1) * 128], PTw[:, j, tq * 128:(tq + 1) * 128], msk, ALU.mult,
                       [('PTw', pb, j), 'cc', 'm4b'], [('PTw', pb, j)])
            for tq in range(4):
                tb = 4 * qt + tq
                ob = nob % 2
                nob += 1
                bs, bw = 4 + ob, 6 + ob
                nk = tb + 1
                for kt in range(nk):
                    MM(P, C.psf[bs][:, 0:65], PTs[:, kt, tq * 128:(tq + 1) * 128], vs[:, kt, :], kt == 0, kt == nk - 1,
                       [('PTs', pb, kt), 'vs'], [('ps', bs)])
                wk = [kt for kt in range(tb - 4, tb + 1) if kt >= 0]
                for i, kt in enumerate(wk):
                    j = kt - (4 * qt - 4)
                    MM(P, C.psf[bw][:, 0:65], PTw[:, j, tq * 128:(tq + 1) * 128], vw[:, kt, :], i == 0, i == len(wk) - 1,
                       [('PTw', pb, j), 'vw'], [('ps', bw)])
                P.op('vector', (lambda o, i: lambda e: e.reciprocal(o, i))(sm[ob][:, 0:1], C.psf[bs][:, 64:65]), [('ps', bs)], [('sm', ob)])
                P.op('vector', (lambda o, i: lambda e: e.reciprocal(o, i))(sm[ob][:, 1:2], C.psf[bw][:, 64:65]), [('ps', bw)], [('sm', ob)])
                TT(P, 'vector', sm[ob][:, 0:1], sm[ob][:, 0:1], gl[:, tb, 3 * h + 1:3 * h + 2], ALU.mult, [('sm', ob), 'gl'], [('sm', ob)])
                TT(P, 'vector', sm[ob][:, 1:2], sm[ob][:, 1:2], gl[:, tb, 3 * h + 2:3 * h + 3], ALU.mult, [('sm', ob), 'gl'], [('sm', ob)])
                P.dma('sync', onl[ob][:, 0:64], C.onsa[tb * 128:(tb + 1) * 128, h * 64:(h + 1) * 64], r=[('onsa', tb)], w=[('onl', ob)])
                STT(P, on[ob][:, 0:64], C.psf[bs][:, 0:64], sm[ob][:, 0:1], onl[ob][:, 0:64], ALU.mult, ALU.add,
                    [('ps', bs), ('sm', ob), ('onl', ob)], [('on', ob)])
                STT(P, on[ob][:, 0:64], C.psf[bw][:, 0:64], sm[ob][:, 1:2], on[ob][:, 0:64], ALU.mult, ALU.add,
                    [('ps', bw), ('sm', ob), ('on', ob)], [('on', ob)])
                P.dma('sync', C.onsa[tb * 128:(tb + 1) * 128, h * 64:(h + 1) * 64], on[ob][:, 0:64], r=[('on', ob)], w=[('onsa', tb)])
    P.barrier()


def phase_sb(P, C, l):
    kT = P.alloc(4 * S, BF16).rearrange("p (h t) -> p h t", h=4)
    qT = P.alloc(4 * S, BF16).rearrange("p (h t) -> p h t", h=4)
    v = P.alloc(32 * 256, BF16).rearrange("p (a e) -> p a e", a=32)
    for h in range(4):
        P.dma('gpsimd', kT[0:64, h, :], C.hfeat[F_SBK + h * 64:F_SBK + (h + 1) * 64, :], w=['kT'])
        P.dma('gpsimd', qT[0:64, h, :], C.hfeat[F_SBQ + h * 64:F_SBQ + (h + 1) * 64, :], w=['qT'])
    for a_ in range(0, 32, 8):
        P.dma('gpsimd', v[:, a_:a_ + 8, :], C.htok[a_ * 128:(a_ + 8) * 128, T_SBV:T_SBV + 256].rearrange("(a p) e -> p a e", p=128), w=['v'])
    cp = P.alloc(S + 8, F32)
    zeros = P.alloc(512, F32)
    MEMSET(P, 'vector', zeros, 0.0, ['zeros'])
    nms = P.alloc(128, F32)
    TS(P, 'vector', nms, C.mstrict[:], -1.0, 1.0, ALU.mult, ALU.add, ['mstrict'], ['nms'])
    bt = [P.alloc(512, F32) for _ in range(3)]
    om = [P.alloc(520, F32) for _ in range(3)]
    at = [P.alloc(512, BF16) for _ in range(3)]
    aT = [P.alloc(512, BF16).rearrange("p (a t) -> p a t", a=4) for _ in range(3)]
    osb = [P.alloc(256, F32) for _ in range(2)]
    nt_ = 0
    for tb in range(NT):
        ob = tb % 2
        nkeys = 128 * (tb + 1)
        ntile = (nkeys + 511) // 512
        for h in range(4):
            obank = 4 + (tb * 4 + h) % 2
            MEMSET(P, 'gpsimd', cp[:, nkeys + 1:nkeys + 2], 1.0, ['cp'])
            first = True
            for c in range(ntile - 1, -1, -1):
                s0 = 512 * c
                w = min(512, nkeys - s0)
                b = nt_ % 3
                bank = (0, 1, 6)[b]
                nt_ += 1
                MM(P, C.psf[bank][:, 0:w], qT[0:64, h, tb * 128:(tb + 1) * 128], kT[0:64, h, s0:s0 + w], True, True, ['qT', 'kT'], [('ps', bank)])
                ACT(P, bt[b][:, 0:w], C.psf[bank][:, 0:w], AF.Sigmoid, [('ps', bank)], [('bt', b)], scale=SC)
                ACT(P, om[b][:, 1:w + 1], C.psf[bank][:, 0:w], AF.Sigmoid, [('ps', bank)], [('om', b)], scale=-SC)
                if c == ntile - 1:
                    TT(P, 'vector', bt[b][:, w - 128:w], bt[b][:, w - 128:w], C.mstrict[:], ALU.mult, [('bt', b), 'mstrict'], [('bt', b)])
                    TT(P, 'vector', om[b][:, w - 127:w + 1], om[b][:, w - 127:w + 1], nms, ALU.max, [('om', b), 'nms'], [('om', b)])
                P.op('vector', (lambda o, d0, d1, ini: lambda e: e.tensor_tensor_scan(out=o, data0=d0, data1=d1, initial=ini, op0=ALU.mult, op1=ALU.add))(
                    cp[:, s0 + w:s0:-1], om[b][:, w:0:-1], zeros[:, 0:w], cp[:, s0 + w + 1:s0 + w + 2]),
                    [('om', b), 'zeros', 'cp'], ['cp'])
                TT(P, 'gpsimd', at[b][:, 0:w], bt[b][:, 0:w], cp[:, s0 + 2:s0 + w + 2], ALU.mult, [('bt', b), 'cp'], [('at', b)])
                tbank = (2, 3, 7)[b]
                nblk = w // 128
                for j in range(nblk):
                    TR(P, C.psb[tbank][:, j * 128:(j + 1) * 128], at[b][:, j * 128:(j + 1) * 128], C.identb[:], [('at', b)], [('ps', tbank)])
                CP(P, 'scalar', aT[b][:, 0:nblk, :], C.psb[tbank][:, 0:w].rearrange("p (a t) -> p a t", a=nblk), [('ps', tbank)], [('aT', b)])
                for j in range(nblk):
                    sblk = s0 // 128 + j
                    last = (c == 0 and j == nblk - 1)
                    MM(P, C.psf[obank][:, 0:64], aT[b][:, j, :], v[:, sblk, h * 64:(h + 1) * 64], first, last, [('aT', b), 'v'], [('ps', obank)])
                    first = False
            CP(P, 'vector', osb[ob][:, h * 64:(h + 1) * 64], C.psf[obank][:, 0:64], [('ps', obank)], [('osb', ob, h)])
        P.dma('sync', C.osb[tb * 128:(tb + 1) * 128, :], osb[ob], r=[('osb', ob, h) for h in range(4)], w=['osbd'])
    P.barrier()


def phase_hg(P, C, l):
    mb = P.alloc(128, F32)
    md = P.alloc(128, F32)
    mk = P.alloc(128, F32)
    ci = P.alloc(2, F32)
    for nm_, t_ in (('hg_mb', mb), ('hg_md', md), ('hg_mk', mk), ('hg_ci', ci)):
        P.dma('sync', t_, C.d[nm_], w=[nm_])
    lbt = P.alloc(2 * 384, F32).rearrange("p (a c) -> p a c", a=2)
    P.dma('sync', lbt, C.d['hg_lb_b'], w=['lbt'])
    oml = P.alloc(384, F32)
    if l == 0:
        MEMSET(P, 'vector', oml, 1.0, ['oml'])
    else:
        TT(P, 'vector', oml, lbt[:, 0, :], lbt[:, 1, :], ALU.subtract, ['lbt'], ['oml'])
        ACT(P, oml, oml, AF.Sigmoid, ['oml'], ['oml'])
    nw = P.alloc(2 * 384, F32).rearrange("p (a c) -> p a c", a=2)
    P.dma('sync', nw, C.d['hg_nw_b'], w=['nw'])
    St = P.alloc(192, F32)
    MEMSET(P, 'vector', St, 0.0, ['St'])
    Sb = [P.alloc(192, BF16) for _ in range(2)]
    CP(P, 'vector', Sb[0], St, ['St'], [('Sb', 0)])
    inp = [P.alloc(3 * 384, F32).rearrange("p (a c) -> p a c", a=3) for _ in range(2)]
    key = P.alloc(384, F32)
    lf = P.alloc(384, F32)
    ex = P.alloc(4 * 384, F32).rearrange("p (a c) -> p a c", a=4)
    qk = P.alloc(4 * 384, BF16).rearrange("p (a c) -> p a c", a=4)
    vb = P.alloc(384, BF16)
    tT = P.alloc(3 * 3 * 128, BF16).rearrange("p (a j t) -> p a j t", a=3, j=3)
    am = P.alloc(6 * 128, BF16).rearrange("p (h t) -> p h t", h=6)
    ebl = P.alloc(6, F32)
    tmpS = P.alloc(192, F32)
    ot = [P.alloc(384, F32) for _ in range(2)]
    sq = P.alloc(384, F32)
    ms = P.alloc(8, F32)
    S3 = lambda a: a.rearrange("p (j e) -> p j e", j=3)
    eb3 = ebl.rearrange("p (j c) -> p j c", c=2)
    for tt in range(NT):
        b = tt % 2
        rows = slice(tt * 128, (tt + 1) * 128)
        P.dma('sync', inp[b][:, 0, :], C.htok[rows, T_HGQ:T_HGQ + 384], w=[('inp', b)])
        P.dma('sync', inp[b][:, 1, :], C.htok[rows, T_HGF:T_HGF + 384], w=[('inp', b)])
        P.dma('sync', inp[b][:, 2, :], C.htok[rows, T_HGI:T_HGI + 384], w=[('inp', b)])
        ik = [('inp', b)]
        ACT(P, key, inp[b][:, 1, :], AF.Sigmoid, ik, ['key'], scale=-1.0)
        TT(P, 'vector', key, key, oml, ALU.mult, ['key', 'oml'], ['key'])
        ACT(P, lf, key, AF.Ln, ['key'], ['lf'], scale=-1.0, bias=C.eps6[:, 2:3])
        CP(P, 'gpsimd', vb, inp[b][:, 2, :], ik, ['vb'])
        MM(P, C.psf[0][:, 0:384], md, lf, True, True, ['hg_md', 'lf'], [('ps', 0)])
        MM(P, C.psf[1][:, 0:384], mb, lf, True, True, ['hg_mb', 'lf'], [('ps', 1)])
        MM(P, C.psf[2][:, 0:384], mk, lf, True, True, ['hg_mk', 'lf'], [('ps', 2)])
        for j in range(3):
            MM(P, C.psf[3][:, 2 * j:2 * j + 2], lf[:, j * 128:(j + 1) * 128], ci, True, True, ['lf', 'hg_ci'], [('ps', 3)])
        ACT(P, ex[:, 0, :], C.psf[0][:, 0:384], AF.Exp, [('ps', 0)], ['ex0'])
        ACT(P, ex[:, 1, :], C.psf[0][:, 0:384], AF.Exp, [('ps', 0)], ['ex1'], scale=-1.0)
        ACT(P, ex[:, 2, :], C.psf[1][:, 0:384], AF.Exp, [('ps', 1)], ['ex2'])
        ACT(P, ex[:, 3, :], C.psf[2][:, 0:384], AF.Exp, [('ps', 2)], ['ex3'])
        ACT(P, ebl, C.psf[3][:, 0:6], AF.Exp, [('ps', 3)], ['ebl'])
        TT(P, 'vector', qk[:, 0, :], inp[b][:, 0, :], ex[:, 0, :], ALU.mult, ik + ['ex0'], ['qk0'])
        TT(P, 'gpsimd', qk[:, 1, :], key, ex[:, 1, :], ALU.mult, ['key', 'ex1'], ['qk1'])
        TT(P, 'vector', qk[:, 2, :], inp[b][:, 0, :], ex[:, 2, :], ALU.mult, ik + ['ex2'], ['qk2'])
        TT(P, 'gpsimd', qk[:, 3, :], key, ex[:, 3, :], ALU.mult, ['key', 'ex3'], ['qk3'])
        for a in range(3):
            pbk = C.psb[4 + a % 2]
            for j in range(3):
                TR(P, pbk[:, j * 128:(j + 1) * 128], qk[:, a, j * 128:(j + 1) * 128], C.identb[:], [f'qk{a}'], [('ps', 4 + a % 2)])
            CP(P, 'scalar' if a != 1 else 'vector', tT[:, a, :, :], pbk[:, 0:384].rearrange("p (j t) -> p j t", j=3), [('ps', 4 + a % 2)], [f'tT{a}'])
        for h in range(6):
            hh, j = h % 2, h // 2
            pr = slice(hh * 64, hh * 64 + 64)
            MM(P, C.psf[6 + hh][:, j * 128:(j + 1) * 128], tT[pr, 1, j, :], tT[pr, 0, j, :], True, True, ['tT0', 'tT1'], [('ps', 6 + hh)])
        am4 = am.rearrange("p (j hh) t -> p j hh t", hh=2)
        for hh in range(2):
            STT(P, am4[:, :, hh, :], C.psf[6 + hh][:, 0:384].rearrange("p (j t) -> p j t", j=3), 1e30, mb.unsqueeze(1).to_broadcast([128, 3, 128]),
                ALU.min, ALU.mult, [('ps', 6 + hh), 'hg_mb'], [f'am{hh}'])
        for c in range(2):
            cr = slice(c * 64, (c + 1) * 64)
            for h in range(6):
                hh, j = h % 2, h // 2
                MM(P, C.psf[c][hh * 64:(hh + 1) * 64, j * 64:(j + 1) * 64], qk[cr, 3, h * 64:(h + 1) * 64], vb[cr, h * 64:(h + 1) * 64],
                   True, True, ['qk3', 'vb'], [('ps', c)])
        TT(P, 'vector', S3(tmpS), S3(St), eb3[:, :, 0:1].to_broadcast([128, 3, 64]), ALU.mult, ['St', 'ebl'], ['tmpS'])
        TT(P, 'vector', St, tmpS, C.psf[0][:, 0:192], ALU.add, ['tmpS', ('ps', 0)], ['St'])
        CP(P, 'vector', Sb[1], St, ['St'], [('Sb', 1)])
        for h in range(6):
            hh, j = h % 2, h // 2
            pr = slice(hh * 64, hh * 64 + 64)
            oc = slice(h * 64, (h + 1) * 64)
            ob = 2 + hh
            pc = slice(j * 64, (j + 1) * 64)
            MM(P, C.psf[ob][:, pc], am[:, h, :], vb[:, oc], True, False, ['am0', 'am1', 'vb'], [('ps', ob)])
            MM(P, C.psf[ob][0:64, pc], tT[pr, 2, j, 0:64], Sb[0][pr, j * 64:(j + 1) * 64], False, True, ['tT2', ('Sb', 0)], [('ps', ob)])
            MM(P, C.psf[ob][64:128, pc], tT[pr, 2, j, 64:128], Sb[1][pr, j * 64:(j + 1) * 64], False, True, ['tT2', ('Sb', 1)], [('ps', ob)])
        TT(P, 'vector', S3(tmpS), S3(St), eb3[:, :, 1:2].to_broadcast([128, 3, 64]), ALU.mult, ['St', 'ebl'], ['tmpS'])
        TT(P, 'vector', St, tmpS, C.psf[1][:, 0:192], ALU.add, ['tmpS', ('ps', 1)], ['St'])
        CP(P, 'vector', Sb[0], St, ['St'], [('Sb', 0)])
        ot4 = ot[b].rearrange("p (j hh e) -> p j hh e", hh=2, e=64)
        CP(P, 'scalar', ot4[:, :, 0, :], C.psf[2][:, 0:192].rearrange("p (j e) -> p j e", j=3), [('ps', 2)], [('ot', b)])
        CP(P, 'scalar', ot4[:, :, 1, :], C.psf[3][:, 0:192].rearrange("p (j e) -> p j e", j=3), [('ps', 3)], [('ot', b)])
        TT(P, 'gpsimd', sq, ot[b], ot[b], ALU.mult, [('ot', b)], ['sq'])
        P.op('vector', (lambda o, i: lambda e: e.tensor_reduce(out=o, in_=i, axis=AX.X, op=ALU.add))(ms[:, 0:6], sq.rearrange("p (h e) -> p h e", h=6)), ['sq'], ['ms'])
        ACT(P, ms[:, 0:6], ms[:, 0:6], AF.Sqrt, ['ms'], ['ms'], scale=1.0 / 64.0, bias=C.eps6[:, 0:1])
        P.op('vector', (lambda o, i: lambda e: e.reciprocal(o, i))(ms[:, 0:6], ms[:, 0:6]), ['ms'], ['ms'])
        TT(P, 'vector', ot[b].rearrange("p (h e) -> p h e", h=6), ot[b].rearrange("p (h e) -> p h e", h=6),
           ms[:, 0:6].unsqueeze(2).to_broadcast([128, 6, 64]), ALU.mult, [('ot', b), 'ms'], [('ot', b)])
        TT(P, 'vector', ot[b], ot[b], nw[:, l, :], ALU.mult, [('ot', b), 'nw'], [('ot', b)])
        P.dma('gpsimd', C.ohg[rows, :], ot[b], r=[('ot', b)], w=['ohgd'])
    P.barrier()


def phase_out(P, C, l, xsrc, xdst):
    W = P.alloc(8 * DM, BF16).rearrange("p (k c) -> p k c", k=8)
    wsrc = C.d['w_out'][l].rearrange("(k p) c -> p k c", p=128)
    for k in range(8):
        P.dma('gpsimd', W[:, k, :], wsrc[:, k, :], w=[('W', k)])
    wkeys = [('W', k) for k in range(8)]
    lng = P.alloc(DM, F32)
    lnb = P.alloc(DM, F32)
    P.dma('sync', lng, C.d['lng_b'][:, l, :], w=['lng'])
    P.dma('sync', lnb, C.d['lnb_b'][:, l, :], w=['lnb'])
    o = [P.alloc(DM, F32) for _ in range(2)]
    z = [P.alloc(DM, F32) for _ in range(2)]
    xt = [P.alloc(DM, F32) for _ in range(2)]
    mx = P.alloc(DM, BF16)
    mT = P.alloc(8 * 128, BF16).rearrange("p (k t) -> p k t", k=8)
    pre = [P.alloc(DM, F32) for _ in range(2)]
    st = P.alloc(16, F32)
    for tt in range(NT):
        b = tt % 2
        rows = slice(tt * 128, (tt + 1) * 128)
        P.dma('sync', o[b][:, 0:384], C.onsa[rows, :], w=[('o', b)])
        P.dma('sync', o[b][:, 384:640], C.osb[rows, :], w=[('o', b)])
        P.dma('sync', o[b][:, 640:1024], C.ohg[rows, :], w=[('o', b)])
        P.dma('sync', z[b][:, 0:384], C.htok[rows, T_Z:T_Z + 384], w=[('z', b)])
        P.dma('sync', z[b][:, 384:640], C.htok[rows, T_SBZ:T_SBZ + 256], w=[('z', b)])
        P.dma('sync', z[b][:, 640:1024], C.htok[rows, T_HGZ:T_HGZ + 384], w=[('z', b)])
        P.dma('sync', xt[b], xsrc[rows, :], w=[('xt', b)])
        ACT(P, z[b], z[b], AF.Silu, [('z', b)], [('z', b)])
        TT(P, 'vector', mx, o[b], z[b], ALU.mult, [('o', b), ('z', b)], ['mx'])
        pbk = C.psb[4]
        for k in range(8):
            TR(P, pbk[:, k * 128:(k + 1) * 128], mx[:, k * 128:(k + 1) * 128], C.identb[:], ['mx'], [('ps', 4)])
        CP(P, 'scalar', mT[:, 0:4, :], pbk[:, 0:512].rearrange("p (k t) -> p k t", k=4), [('ps', 4)], ['mT'])
        CP(P, 'vector', mT[:, 4:8, :], pbk[:, 512:1024].rearrange("p (k t) -> p k t", k=4), [('ps', 4)], ['mT'])
        for nh in range(2):
            bank = (tt * 2 + nh) % 4
            for k in range(8):
                MM(P, C.psf[bank][:, :], mT[:, k, :], W[:, k, nh * 512:(nh + 1) * 512], k == 0, k == 7, ['mT'] + wkeys, [('ps', bank)])
            STT(P, pre[b][:, nh * 512:(nh + 1) * 512], xt[b][:, nh * 512:(nh + 1) * 512], ALPHA, C.psf[bank][:, :], ALU.mult, ALU.add,
                [('xt', b), ('ps', bank)], [('pre', b, nh)])
            P.op('vector', (lambda o_, i_: lambda e: e.bn_stats(o_, i_))(st[:, nh * 6:(nh + 1) * 6], pre[b][:, nh * 512:(nh + 1) * 512]),
                 [('pre', b, nh)], ['st'])
        P.op('vector', (lambda o_, i_: lambda e: e.bn_aggr(o_, i_))(st[:, 12:14], st[:, 0:12]), ['st'], ['st'])
        ACT(P, st[:, 14:15], st[:, 13:14], AF.Sqrt, ['st'], ['st'], bias=C.eps6[:, 1:2])
        P.op('vector', (lambda o_, i_: lambda e: e.reciprocal(o_, i_))(st[:, 14:15], st[:, 14:15]), ['st'], ['st'])
        pk = [('pre', b, 0), ('pre', b, 1)]
        TS(P, 'vector', pre[b], pre[b], st[:, 12:13], st[:, 14:15], ALU.subtract, ALU.mult, pk + ['st'], pk)
        TT(P, 'gpsimd', pre[b], pre[b], lng, ALU.mult, pk + ['lng'], pk)
        TT(P, 'gpsimd', pre[b], pre[b], lnb, ALU.add, pk + ['lnb'], pk)
        P.dma('gpsimd', xdst[rows, :], pre[b], r=pk, w=['xdst'])
    P.barrier()


def build(debug=False, layers=(0, 1), phases=None, ext=()):
    nc = bass.Bass("TRN2", target_bir_lowering=False)
    P = Prog(nc)
    C = Ctx()
    C.d = {}
    for k, shp in list(IN_SHAPES.items()) + list(CONST_SHAPES.items()):
        C.d[k] = nc.dram_tensor(k, shp, F32, kind="ExternalInput").ap()
    skind = "ExternalOutput" if debug else "Internal"
    sk = lambda n: ("ExternalInput" if n in ext else skind)
    C.htok = nc.dram_tensor("htok", [S, NTOK], F32, kind=sk("htok")).ap()
    C.hfeat = nc.dram_tensor("hfeat", [NFEAT, S], F32, kind=sk("hfeat")).ap()
    C.onsa = nc.dram_tensor("onsa", [S, 384], F32, kind=sk("onsa")).ap()
    C.osb = nc.dram_tensor("osb", [S, 256], F32, kind=sk("osb")).ap()
    C.ohg = nc.dram_tensor("ohg", [S, 384], F32, kind=sk("ohg")).ap()
    C.x1 = nc.dram_tensor("x1", [S, DM], F32, kind=skind).ap()
    C.out = nc.dram_tensor("out", [S, DM], F32, kind="ExternalOutput").ap()
    C.identf = P.sb([128, 128], F32, "identf")
    C.identb = P.sb([128, 128], BF16, "identb")
    C.m0 = P.sb([128, 128], F32, "m0")
    C.m4 = P.sb([128, 128], F32, "m4")
    C.m4b = P.sb([128, 128], BF16, "m4b")
    C.mstrict = P.sb([128, 128], F32, "mstrict")
    C.eps6 = P.sb([128, 4], F32, "eps6")
    C.kcT = P.sb([64, 2, 256], F32, "kcT")
    C.vc = P.sb([128, 2, 2, 64], F32, "vc")
    C.NM = [P.sb([64, S], BF16, f"NM{g}") for g in range(2)]
    C.psf = [P.ps([128, 512], F32, f"psum{i}") for i in range(8)]
    C.psb = [p[:].bitcast(BF16) for p in C.psf]
    P.make_arena(38 * 1024)
    P.dma('sync', C.identf[:], C.d['ident'], w=['identf'])
    P.dma('sync', C.m0[:], C.d['m0'], w=['m0'])
    P.dma('sync', C.m4[:], C.d['m4'], w=['m4'])
    P.dma('sync', C.mstrict[:], C.d['mstrict'], w=['mstrict'])
    CP(P, 'vector', C.identb[:], C.identf[:], ['identf'], ['identb'])
    CP(P, 'vector', C.m4b[:], C.m4[:], ['m4'], ['m4b'])
    MEMSET(P, 'vector', C.eps6[:, 0:1], 1e-6, ['eps6'])
    MEMSET(P, 'vector', C.eps6[:, 1:2], 1e-5, ['eps6'])
    MEMSET(P, 'vector', C.eps6[:, 2:3], 1.0, ['eps6'])
    MEMSET(P, 'vector', C.eps6[:, 3:4], 0.0, ['eps6'])
    MEMSET(P, 'vector', C.kcT[:], 0.0, ['kcT'])
    MEMSET(P, 'vector', C.vc[:], 0.0, ['vc'])
    P.barrier()
    allp = ('proj', 'cmp_mlp', 'cmp_att', 'slcwin', 'sb', 'hg', 'out')
    phases = phases or allp
    for l in layers:
        xsrc = C.d['x'] if l == 0 else C.x1
        xdst = C.out if l == DEPTH - 1 else C.x1
        if 'proj' in phases:
            phase_proj(P, C, l, xsrc)
        if 'cmp_mlp' in phases:
            phase_cmp_mlp(P, C, l)
        if 'cmp_att' in phases:
            phase_cmp_att(P, C, l)
        if 'slcwin' in phases:
            phase_slcwin(P, C, l)
        if 'sb' in phases:
            phase_sb(P, C, l)
        if 'hg' in phases:
            phase_hg(P, C, l)
        if 'out' in phases:
            phase_out(P, C, l, xsrc, xdst)
    P.emit()
    return nc, P


def make_in_maps(inputs, cores=range(8)):
    consts = host_consts()
    consts.update(host_gathers(np.asarray(inputs['rel_bias'], np.float32)))
    f = lambda a: np.ascontiguousarray(np.asarray(a, np.float32))
    shared = {
        'w_in': f(inputs['w_in']), 'cmp_posT': f(np.asarray(inputs['cmp_pos']).transpose(0, 2, 1)),
        'w_ck1': f(inputs['w_ck1']), 'w_ck2': f(inputs['w_ck2']), 'w_cv1': f(inputs['w_cv1']), 'w_cv2': f(inputs['w_cv2']),
        'hg_lb_b': f(np.broadcast_to(np.asarray(inputs['hg_lb'])[None], (128, DEPTH, 384))),
        'hg_nw_b': f(np.broadcast_to(np.asarray(inputs['hg_norm_w'])[None], (128, DEPTH, 384))),
        'w_out': f(inputs['w_out']),
        'lng_b': f(np.broadcast_to(np.asarray(inputs['ln_g'])[None], (128, DEPTH, DM))),
        'lnb_b': f(np.broadcast_to(np.asarray(inputs['ln_b'])[None], (128, DEPTH, DM))),
    }
    for k, v in consts.items():
        shared[k] = f(v)
    x = np.asarray(inputs['x'], np.float32)
    maps = []
    for c in cores:
        m = dict(shared)
        m['x'] = np.ascontiguousarray(x[c])
        maps.append(m)
    return maps


_CACHE = {}


def kernel(**inputs):
    if 'nc' not in _CACHE:
        _CACHE['nc'] = build()[0]
    nc = _CACHE['nc']
    maps = make_in_maps(inputs)
    res = run_bass_kernel_spmd(nc, maps, core_ids=list(range(8)))
    out = np.stack([np.asarray(r['out'], np.float32) for r in res.results], axis=0)
    return out
```
